# Optimizing a Trainium2 kernel written in Bass

```python
import math
import jax, jax.numpy as jnp
from jax import lax
import numpy as np

D_MODEL = 1024
BATCH = 8
SEQ = 2048
DEPTH = 1

GRID_W = 64
CTX_LEN = 256
LRU_WIDTH = 1024
LRU_HEADS = 8
LRU_HEAD_DIM = LRU_WIDTH // LRU_HEADS
LRU_C = 8.0
CONV_WIDTH = 4
CONV_PAD_LEFT = 2
CONV_PAD_RIGHT = CONV_WIDTH - 1 - CONV_PAD_LEFT
FOURIER_WIDTH = 512
FOURIER_GROUPS = 4
FOURIER_GROUP_DIM = FOURIER_WIDTH // FOURIER_GROUPS
N_BRANCHES = 2
IN_COLS = 2 * LRU_WIDTH + FOURIER_WIDTH + N_BRANCHES * D_MODEL
D_FF = 4 * D_MODEL
N_MOD = 6
EPS = 1e-6
POS_MAX_PERIOD = 10000.0

kernel_name = "hybrid_rglru_fnet_dit_block"


def rms_norm(x, g):
    xf = x.astype(jnp.float32)
    y = xf * lax.rsqrt(jnp.mean(xf * xf, axis=-1, keepdims=True) + EPS)
    return (y * g.astype(jnp.float32)).astype(x.dtype)


def modulate(h, shift, scale):
    return h * (1 + scale) + shift


def sincos_1d(pos, dim):
    half = dim // 2
    freqs = jnp.exp(-math.log(POS_MAX_PERIOD) * jnp.arange(half, dtype=jnp.float32) / half)
    ang = pos[:, None] * freqs[None, :]
    return jnp.concatenate([jnp.sin(ang), jnp.cos(ang)], axis=-1)


def grid_pos_embed(n_tokens, dtype):
    rows = n_tokens // GRID_W
    er = sincos_1d(jnp.arange(rows, dtype=jnp.float32), D_MODEL // 2)
    ec = sincos_1d(jnp.arange(GRID_W, dtype=jnp.float32), D_MODEL // 2)
    emb = jnp.concatenate([
        jnp.broadcast_to(er[:, None, :], (rows, GRID_W, D_MODEL // 2)),
        jnp.broadcast_to(ec[None, :, :], (rows, GRID_W, D_MODEL // 2)),
    ], axis=-1)
    return emb.reshape(rows * GRID_W, D_MODEL).astype(dtype)


def depthwise_conv(u, w, b):
    t = u.shape[1]
    up = jnp.pad(u, ((0, 0), (CONV_PAD_LEFT, CONV_PAD_RIGHT), (0, 0)))
    y = b
    for k in range(CONV_WIDTH):
        y = y + up[:, k:k + t] * w[k]
    return y


def rglru_coeffs(xc, w_a, b_a, w_x, b_x, lam):
    bsz, t, _ = xc.shape
    xh = xc.reshape(bsz, t, LRU_HEADS, LRU_HEAD_DIM)
    r = jax.nn.sigmoid(jnp.einsum('bthd,nhde->nbthe', xh, w_a).reshape(2, bsz, t, LRU_WIDTH)
                       + b_a[:, None, None, :])
    i = jax.nn.sigmoid(jnp.einsum('bthd,nhde->nbthe', xh, w_x).reshape(2, bsz, t, LRU_WIDTH)
                       + b_x[:, None, None, :])
    log_a = -LRU_C * r.astype(jnp.float32) * jax.nn.softplus(-lam.astype(jnp.float32))[:, None, None, :]
    a = jnp.exp(log_a)
    mult = jnp.sqrt(-jnp.expm1(2.0 * log_a))
    bterm = mult * (i * xc[None]).astype(jnp.float32)
    return a, bterm


def linear_scan(a, b, h0, reverse):
    def combine(l, r):
        a_l, b_l = l
        a_r, b_r = r
        return a_l * a_r, a_r * b_l + b_r
    a_cum, h = lax.associative_scan(combine, (a, b), reverse=reverse, axis=1)
    return h + a_cum * h0[:, None, :]


def bidir_rglru(xc_ctx, xc_lat, w_a, b_a, w_x, b_x, lam):
    a_c, b_c = rglru_coeffs(xc_ctx, w_a, b_a, w_x, b_x, lam)
    zeros = jnp.zeros((xc_ctx.shape[0], LRU_WIDTH), jnp.float32)
    h_cf = linear_scan(a_c[0], b_c[0], zeros, reverse=False)
    h_cb = linear_scan(a_c[1], b_c[1], zeros, reverse=True)
    a_l, b_l = rglru_coeffs(xc_lat, w_a, b_a, w_x, b_x, lam)
    h_lf = linear_scan(a_l[0], b_l[0], h_cf[:, -1], reverse=False)
    h_lb = linear_scan(a_l[1], b_l[1], h_cb[:, 0], reverse=True)
    return h_cf, h_cb, (h_lf + h_lb).astype(xc_lat.dtype)


def fourier_mix(u):
    bsz, t, _ = u.shape
    ug = u.astype(jnp.float32).reshape(bsz, t, FOURIER_GROUPS, FOURIER_GROUP_DIM)
    y = jnp.fft.fft2(ug, axes=(1, 3), norm="ortho").real
    return y.reshape(bsz, t, FOURIER_WIDTH).astype(u.dtype)


def split_in_cols(u):
    ux = u[..., :LRU_WIDTH]
    uy = u[..., LRU_WIDTH:2 * LRU_WIDTH]
    uf = u[..., 2 * LRU_WIDTH:2 * LRU_WIDTH + FOURIER_WIDTH]
    ug = u[..., 2 * LRU_WIDTH + FOURIER_WIDTH:]
    return ux, uy, uf, ug


def branch_merge(y_lru, uy, uf, ug, w_lru_out, w_f_out, w_out):
    y_a = (y_lru * jax.nn.gelu(uy)) @ w_lru_out
    y_b = fourier_mix(uf) @ w_f_out
    g = jax.nn.sigmoid(ug)
    merged = g[..., :D_MODEL] * y_a + g[..., D_MODEL:] * y_b
    return merged @ w_out


def sq_relu_mlp(h, w1, w2):
    return jnp.square(jax.nn.relu(h @ w1)) @ w2


def setup_inputs(seed: int = 0) -> dict:
    key = jax.random.key(seed)
    ks = jax.random.split(key, 24)
    f32 = jnp.float32
    nrm = lambda k, shape, fan_in: jax.random.normal(k, shape, f32) * (fan_in ** -0.5)
    u = jax.random.uniform(ks[12], (DEPTH, 2, LRU_WIDTH), f32, minval=0.9, maxval=0.999)
    s = u ** (1.0 / LRU_C)
    lam = jnp.log(s) - jnp.log1p(-s)
    return {
        "x": jax.random.normal(ks[0], (BATCH, SEQ, D_MODEL), f32),
        "c": jax.random.normal(ks[1], (BATCH, D_MODEL), f32),
        "ctx": jax.random.normal(ks[2], (BATCH, CTX_LEN, D_MODEL), f32),
        "c_ctx": jax.random.normal(ks[3], (D_MODEL,), f32),
        "w_mod": nrm(ks[4], (DEPTH, D_MODEL, N_MOD * D_MODEL), D_MODEL),
        "b_mod": 0.01 * jax.random.normal(ks[5], (DEPTH, N_MOD * D_MODEL), f32),
        "g_mix": 1.0 + 0.05 * jax.random.normal(ks[6], (DEPTH, D_MODEL), f32),
        "w_in": nrm(ks[7], (DEPTH, D_MODEL, IN_COLS), D_MODEL),
        "conv_w": nrm(ks[8], (DEPTH, CONV_WIDTH, LRU_WIDTH), CONV_WIDTH),
        "conv_b": 0.01 * jax.random.normal(ks[9], (DEPTH, LRU_WIDTH), f32),
        "w_a": nrm(ks[10], (DEPTH, 2, LRU_HEADS, LRU_HEAD_DIM, LRU_HEAD_DIM), LRU_HEAD_DIM),
        "b_a": 0.01 * jax.random.normal(ks[11], (DEPTH, 2, LRU_WIDTH), f32),
        "w_x": nrm(ks[13], (DEPTH, 2, LRU_HEADS, LRU_HEAD_DIM, LRU_HEAD_DIM), LRU_HEAD_DIM),
        "b_x": 0.01 * jax.random.normal(ks[14], (DEPTH, 2, LRU_WIDTH), f32),
        "lam": lam,
        "w_lru_out": nrm(ks[15], (DEPTH, LRU_WIDTH, D_MODEL), LRU_WIDTH),
        "w_f_out": nrm(ks[16], (DEPTH, FOURIER_WIDTH, D_MODEL), FOURIER_WIDTH),
        "w_out": nrm(ks[17], (DEPTH, D_MODEL, D_MODEL), D_MODEL),
        "g_mlp": 1.0 + 0.05 * jax.random.normal(ks[18], (DEPTH, D_MODEL), f32),
        "w1": nrm(ks[19], (DEPTH, D_MODEL, D_FF), D_MODEL),
        "w2": nrm(ks[20], (DEPTH, D_FF, D_MODEL), D_FF),
        "g_final": 1.0 + 0.05 * jax.random.normal(ks[21], (D_MODEL,), f32),
    }


def reference(x, c, ctx, c_ctx, w_mod, b_mod, g_mix, w_in, conv_w, conv_b, w_a, b_a, w_x, b_x,
              lam, w_lru_out, w_f_out, w_out, g_mlp, w1, w2, g_final):
    n_lat = x.shape[1]
    x = x + grid_pos_embed(n_lat, x.dtype)[None]
    for i in range(DEPTH):
        last = i == DEPTH - 1
        mod_l = [m[:, None, :] for m in jnp.split(jax.nn.silu(c) @ w_mod[i] + b_mod[i], N_MOD, axis=-1)]
        mod_c = jnp.split(jax.nn.silu(c_ctx) @ w_mod[i] + b_mod[i], N_MOD, axis=-1)
        sh1, sc1, gt1, sh2, sc2, gt2 = mod_l
        sh1c, sc1c, gt1c, sh2c, sc2c, gt2c = mod_c

        h_lat = modulate(rms_norm(x, g_mix[i]), sh1, sc1)
        h_ctx = modulate(rms_norm(ctx, g_mix[i]), sh1c, sc1c)
        ux_l, uy_l, uf_l, ug_l = split_in_cols(h_lat @ w_in[i])
        if last:
            ux_c = h_ctx @ w_in[i][:, :LRU_WIDTH]
        else:
            ux_c, uy_c, uf_c, ug_c = split_in_cols(h_ctx @ w_in[i])
        xc_l = depthwise_conv(ux_l, conv_w[i], conv_b[i])
        xc_c = depthwise_conv(ux_c, conv_w[i], conv_b[i])
        h_cf, h_cb, y_lru_l = bidir_rglru(xc_c, xc_l, w_a[i], b_a[i], w_x[i], b_x[i], lam[i])
        x = x + gt1 * branch_merge(y_lru_l, uy_l, uf_l, ug_l, w_lru_out[i], w_f_out[i], w_out[i])

        x = x + gt2 * sq_relu_mlp(modulate(rms_norm(x, g_mlp[i]), sh2, sc2), w1[i], w2[i])

        if not last:
            y_lru_c = (h_cf + h_cb).astype(ctx.dtype)
            ctx = ctx + gt1c * branch_merge(y_lru_c, uy_c, uf_c, ug_c, w_lru_out[i], w_f_out[i], w_out[i])
            ctx = ctx + gt2c * sq_relu_mlp(modulate(rms_norm(ctx, g_mlp[i]), sh2c, sc2c), w1[i], w2[i])
    return rms_norm(x, g_final)
```

```python
import math
import numpy as np
import ml_dtypes
import concourse.bass as bass
import concourse.mybir as mybir
from concourse.bass_utils import run_bass_kernel_spmd

F32 = mybir.dt.float32
BF16 = mybir.dt.bfloat16
AF = mybir.ActivationFunctionType
ALU = mybir.AluOpType

PE, ACT, DVE, POOL, SP = "tensor", "scalar", "vector", "gpsimd", "sync"
COMPUTE = (PE, ACT, DVE, POOL)

T = 2048
D = 1024
CT = 256
EPS = 1e-6
NV = 168
DEBUG = False


class Res:
    __slots__ = ("name", "writer", "readers")

    def __init__(self, name):
        self.name = name
        self.writer = None
        self.readers = []


class Op:
    __slots__ = ("eng", "fn", "deps", "rawdeps", "signaled", "sigval", "is_dma", "dsem")

    def __init__(self, eng, fn, is_dma=False):
        self.eng = eng
        self.fn = fn
        self.deps = []
        self.rawdeps = []
        self.signaled = False
        self.sigval = None
        self.is_dma = is_dma
        self.dsem = None


class Prog:
    def __init__(self, nc):
        self.nc = nc
        self.streams = {e: [] for e in (PE, ACT, DVE, POOL, SP)}
        self.dma_sems = {}
        self.last = {e: None for e in COMPUTE}

    def op(self, eng, fn, reads=(), writes=(), dma=False, dkey=None):
        o = Op(eng, fn, is_dma=dma)
        deps = []
        for r in reads:
            if r.writer is not None:
                deps.append(r.writer)
        for w in writes:
            if w.writer is not None:
                deps.append(w.writer)
            deps.extend(w.readers)
        seen = set()
        for d in deps:
            if id(d) in seen or d is o:
                continue
            seen.add(id(d))
            if (not d.is_dma) and d.eng == PE and eng == PE and not dma:
                continue
            if d.is_dma:
                o.rawdeps.append((d.dsem, 16 * d.dsem[1]))
                continue
            o.deps.append(d)
            d.signaled = True
        if dma:
            if dkey is None:
                key = writes[0] if writes else reads[0]
                kid, kname = id(key), key.name
            else:
                kid, kname = dkey, dkey
            ent = self.dma_sems.get(kid)
            if ent is None:
                ent = [None, 0, kname]
                self.dma_sems[kid] = ent
            ent[1] += 1
            o.dsem = ent
            o.sigval = 16 * ent[1]
        else:
            self.last[eng] = o
        for r in reads:
            r.readers.append(o)
        for w in writes:
            w.writer = o
            w.readers = []
        self.streams[eng].append(o)
        return o

    def barrier(self):
        tails = [o for o in self.last.values() if o is not None]
        for o in tails:
            o.signaled = True
        dstate = [(ent, 16 * ent[1]) for ent in self.dma_sems.values()]
        for e in (PE, ACT, DVE, POOL, SP):
            b = Op(e, None)
            b.deps = [t for t in tails if t.eng != e]
            b.rawdeps = list(dstate)
            self.streams[e].append(b)

    def emit(self, final_ops):
        nc = self.nc
        from contextlib import ExitStack
        for e in COMPUTE:
            cnt = 0
            for o in self.streams[e]:
                if o.is_dma or o.fn is None:
                    continue
                if o.signaled:
                    cnt += 1
                    o.sigval = cnt
        with ExitStack() as es:
            tsem = {e: es.enter_context(nc.semaphore("tl_" + e)) for e in COMPUTE}
            for i, ent in enumerate(self.dma_sems.values()):
                ent[0] = es.enter_context(nc.semaphore("d%d_%s" % (i, ent[2])))
            block = es.enter_context(nc.Block())

            def run_stream(e, engobj):
                waited = {}

                def w(sem, val):
                    k = id(sem)
                    if waited.get(k, 0) >= val:
                        return
                    waited[k] = val
                    engobj.wait_ge(sem, val)

                for o in self.streams[e]:
                    for d in o.deps:
                        w(tsem[d.eng], d.sigval)
                    for ent, val in o.rawdeps:
                        if val > 0:
                            w(ent[0], val)
                    if o.fn is None:
                        continue
                    ins = o.fn(engobj)
                    if o.is_dma:
                        ins.then_inc(o.dsem[0], 16)
                    elif o.signaled:
                        ins.then_inc(tsem[e], 1)
                if e == SP:
                    for ent in self.dma_sems.values():
                        w(ent[0], 16 * ent[1])

            block.sync(lambda eng: run_stream(SP, eng))
            block.tensor(lambda eng: run_stream(PE, eng))
            block.scalar(lambda eng: run_stream(ACT, eng))
            block.vector(lambda eng: run_stream(DVE, eng))
            block.gpsimd(lambda eng: run_stream(POOL, eng))


class Arena:
    def __init__(self, nc, nbytes):
        self.nc = nc
        base = nc._sbuf_addr_for_side("left")
        base = (base + 63) // 64 * 64
        slab = nc.alloc_sbuf_tensor("arena", [128, nbytes // 4], F32)
        self.base = base
        self.top = base
        self.end = base + nbytes
        self.n = 0
        self.peak = base

    def alloc(self, name, shape, dt):
        sz = 2 if dt == BF16 else 4
        n = 1
        for s in shape[1:]:
            n *= s
        nbytes = (n * sz + 63) // 64 * 64
        off = self.top
        self.top += nbytes
        assert self.top <= self.end, "SBUF arena overflow at %s: %d > %d" % (name, self.top - self.base, self.end - self.base)
        self.peak = max(self.peak, self.top)
        self.n += 1
        self.last_off = off
        return self.nc.alloc_sbuf_tensor_at("%s_%d" % (name, self.n), shape, dt, offset=off)

    def alloc_abs(self, name, shape, dt, off):
        self.n += 1
        return self.nc.alloc_sbuf_tensor_at("%s_%d" % (name, self.n), shape, dt, offset=off)

    def alloc_top(self, name, shape, dt, off_from_end):
        self.n += 1
        return self.nc.alloc_sbuf_tensor_at("%s_%d" % (name, self.n), shape, dt, offset=self.end - off_from_end)

    def limit(self, reserve):
        assert self.top <= self.end - reserve, "arena reserve violated: %d > %d" % (self.top - self.base, self.end - reserve - self.base)

    def mark(self):
        return self.top

    def release(self, m):
        print("[arena] release at %.1f KiB -> %.1f KiB" % ((self.top - self.base) / 1024.0, (m - self.base) / 1024.0))
        self.top = m


def build_program(debug=False):
    nc = bass.Bass("TRN2", target_bir_lowering=False)

    def din(name, shape, dt=F32):
        return nc.dram_tensor(name, shape, dt, kind="ExternalInput").ap()

    x_d = din("x", [T, D])
    ctx_d = din("ctx", [CT, D])
    pos_d = din("pos", [T, D])
    vecs_d = din("vecs", [128, NV])
    bmod_d = din("bmod", [1, 6 * D])
    gfin_d = din("gfin", [1, D])
    wmod_d = din("wmod", [12, 128, 8, 512])
    win_d = din("win", [36, 128, 8, 128])
    wg_d = din("wg", [8, 128, 4, 128])
    wlru_d = din("wlru", [8, 128, 8, 128])
    wfo_d = din("wfo", [8, 128, 4, 128])
    wout_d = din("wout", [128, 8, D])
    w1_d = din("w1", [32, 128, 8, 128])
    w2_d = din("w2", [2, 8, 128, 4, 512])
    fc_d = din("fc", [128, 256], BF16)
    ct_d = din("ctab", [2, 128, 16, 512], BF16)
    st_d = din("stab", [2, 128, 16, 512], BF16)
    cn_d = din("cnyq", [128, 2], BF16)
    pos2_d = din("pos2", [T, 2, D], BF16)
    ident_d = din("ident", [128, 128], BF16)
    identf_d = din("identf", [128, 128], F32)
    out_d = nc.dram_tensor("out", [T, D], F32, kind="ExternalOutput").ap()
    xp_d = nc.dram_tensor("xp_scr", [T, D], F32, kind="Internal").ap()
    x1_d = nc.dram_tensor("x1_scr", [T, D], F32, kind="Internal").ap()
    gt_d = nc.dram_tensor("gt_scr", [1, 2048], F32, kind="Internal").ap()
    dbg = {}
    if debug:
        dbg["hT"] = nc.dram_tensor("dbg_hT", [128, 8, T], BF16, kind="ExternalOutput").ap()
        dbg["yfT"] = nc.dram_tensor("dbg_yfT", [128, 4, T], BF16, kind="ExternalOutput").ap()
        dbg["ylruT"] = nc.dram_tensor("dbg_ylruT", [128, 8, T], BF16, kind="ExternalOutput").ap()
        dbg["mergedT"] = nc.dram_tensor("dbg_mergedT", [128, 8, T], BF16, kind="ExternalOutput").ap()
        dbg["h2T"] = nc.dram_tensor("dbg_h2T", [128, 8, T], BF16, kind="ExternalOutput").ap()
        dbg["hv"] = nc.dram_tensor("dbg_hv", [128, 128], F32, kind="ExternalOutput").ap()
        dbg["siluc"] = nc.dram_tensor("dbg_siluc", [128, 16], BF16, kind="ExternalOutput").ap()
        dbg["modL"] = nc.dram_tensor("dbg_modL", [128, 32], F32, kind="ExternalOutput").ap()
        dbg["vecs"] = nc.dram_tensor("dbg_vecs", [128, NV], F32, kind="ExternalOutput").ap()
        dbg["gt"] = nc.dram_tensor("dbg_gt", [128, D], F32, kind="ExternalOutput").ap()

    P = Prog(nc)
    A = Arena(nc, 207 * 1024)
    banks = [nc.alloc_psum_tensor("bank%d" % i, [128, 512], F32) for i in range(8)]
    R_bank = [Res("bank%d" % i) for i in range(8)]
    bstate = {"i": 0}
    finals = []

    def nbank():
        i = bstate["i"]
        bstate["i"] = (i + 1) % 8
        return banks[i], R_bank[i]

    def dma(eng, out, in_, reads=(), writes=(), accum=None, sem=None):
        if accum is None:
            return P.op(eng, lambda e: e.dma_start(out=out, in_=in_), reads, writes, dma=True, dkey=sem)
        return P.op(eng, lambda e: e.dma_start(out=out, in_=in_, accum_op=accum), reads, writes, dma=True, dkey=sem)

    def mm(out, lhsT, rhs, start, stop, reads, writes):
        return P.op(PE, lambda e: e.matmul(out, lhsT=lhsT, rhs=rhs, start=start, stop=stop), reads, writes)

    def tr(out, in_, ident, reads, writes):
        return P.op(PE, lambda e: e.transpose(out, in_, ident), reads, writes)

    def act(out, in_, func, reads, writes, bias=None, scale=None, accum=None):
        kw = {}
        if bias is not None:
            kw["bias"] = bias
        if scale is not None:
            kw["scale"] = scale
        if accum is not None:
            kw["accum_out"] = accum
        return P.op(ACT, lambda e: e.activation(out=out, in_=in_, func=func, **kw), reads, writes)

    def ts(eng, out, in0, s1, s2, op0, op1, reads, writes):
        if s2 is None:
            return P.op(eng, lambda e: e.tensor_scalar(out=out, in0=in0, scalar1=s1, scalar2=None, op0=op0), reads, writes)
        return P.op(eng, lambda e: e.tensor_scalar(out=out, in0=in0, scalar1=s1, scalar2=s2, op0=op0, op1=op1), reads, writes)

    def tt(eng, out, in0, in1, op, reads, writes):
        return P.op(eng, lambda e: e.tensor_tensor(out=out, in0=in0, in1=in1, op=op), reads, writes)

    def stt(out, in0, scalar, in1, op0, op1, reads, writes):
        return P.op(DVE, lambda e: e.scalar_tensor_tensor(out=out, in0=in0, scalar=scalar, in1=in1, op0=op0, op1=op1),
                    reads, writes)

    def scan(out, d0, d1, init, reads, writes):
        return P.op(DVE, lambda e: e.tensor_tensor_scan(out=out, data0=d0, data1=d1, initial=init,
                                                        op0=ALU.mult, op1=ALU.add), reads, writes)

    def recip(out, in_, reads, writes):
        return P.op(DVE, lambda e: e.reciprocal(out=out, in_=in_), reads, writes)

    def memset(eng, ap, val, writes):
        return P.op(eng, lambda e: e.memset(ap, val), (), writes)

    def act_fence(writes):
        return P.op(ACT, lambda e: e.activation(out=fence_o[:], in_=fence_t[:], func=AF.Identity), [R_fence], writes)

    def copy(eng, out, in_, reads, writes):
        return P.op(eng, lambda e: e.tensor_copy(out=out, in_=in_), reads, writes)

    STG = {"tiles": [], "R": [], "i": 0, "ce": 0}

    def stg_setup(n):
        STG["tiles"] = [A.alloc("stg", [128, 1024], F32) for _ in range(n)]
        STG["R"] = [Res("stg%d" % i) for i in range(n)]
        STG["i"] = 0

    def load_w(dst, src, a, b, R_dst, engs):
        ga = max(1, 1024 // b)
        a0 = 0
        while a0 < a:
            g = min(ga, a - a0)
            sl = STG["i"] % len(STG["tiles"])
            STG["i"] += 1
            sv = STG["tiles"][sl][:, 0:g * b].rearrange("p (a b) -> p a b", a=g)
            dma(SP, sv, src[:, a0:a0 + g, :], writes=[STG["R"][sl]], sem="stg%d" % sl)
            eng = engs[STG["ce"] % len(engs)]
            STG["ce"] += 1
            if eng == ACT:
                act(dst[:, a0:a0 + g, :], sv, AF.Identity, [STG["R"][sl]], [R_dst])
            else:
                copy(eng, dst[:, a0:a0 + g, :], sv, [STG["R"][sl]], [R_dst])
            a0 += g

    vecs = A.alloc("vecs", [128, NV], F32)
    R_vecs = Res("vecs")
    hv = A.alloc("hv", [128, 128], F32)
    R_hv = Res("hv")
    ident = A.alloc("ident", [128, 128], BF16)
    R_ident = Res("ident")
    identf = A.alloc("identf", [128, 128], F32)
    R_identf = Res("identf")
    fence_t = A.alloc("fence_t", [128, 1], F32)
    fence_o = A.alloc("fence_o", [128, 1], F32)
    R_fence = Res("fence")
    siluc = A.alloc("siluc", [128, 16], BF16)
    R_siluc = Res("siluc")
    modL = A.alloc("modL", [128, 32], F32)
    modC = A.alloc("modC", [128, 16], F32)
    R_mod = Res("mod")
    R_gtd = Res("gtd")
    hcT = A.alloc("hcT", [128, 8, CT], BF16)
    R_hcT = Res("hcT")
    hT = A.alloc("hT", [128, 8, T], BF16)
    R_hT = [Res("hT%d" % g) for g in range(4)]
    MARK_E = A.mark()
    ylruT = A.alloc("ylruT", [128, 8, T], BF16)
    OFF_YLRU = A.last_off
    R_ylru = [Res("ylru%d" % h) for h in range(8)]
    yfT = A.alloc("yfT", [128, 4, T], BF16)
    R_yf = [Res("yf%d" % k) for k in range(4)]

    GMOD1, SH1, GMOD1C, SH1C, GMOD2, SH2 = 0, 8, 16, 24, 32, 40
    HBA, HBX, CS, HCS, NQ, QQ, EPSC = 48, 64, 80, 96, 112, 113, 114
    V_C, V_CC, V_BM, V_GMIX, V_GMLP, V_CW, V_CB, V_BA, V_BX, V_LAM = 0, 8, 16, 64, 72, 80, 112, 120, 136, 152

    dma(SP, vecs[:], vecs_d, writes=[R_vecs], sem="const")
    dma(SP, ident[:], ident_d, writes=[R_ident], sem="const")
    dma(SP, identf[:], identf_d, writes=[R_identf], sem="const")
    memset(DVE, fence_t[:], 0.0, [R_fence])
    memset(DVE, hv[:, NQ:NQ + 1], -0.25, [R_hv])
    memset(DVE, hv[:, QQ:QQ + 1], 0.25, [R_hv])
    memset(DVE, hv[:, EPSC:EPSC + 1], EPS, [R_hv])
    act(siluc[:], vecs[:, 0:16], AF.Silu, [R_vecs], [R_siluc])
    act(hv[:, CS:CS + 16], vecs[:, V_LAM:V_LAM + 16], AF.Exp, [R_vecs], [R_hv], scale=-1.0)
    act(hv[:, CS:CS + 16], hv[:, CS:CS + 16], AF.Ln, [R_hv], [R_hv], bias=1.0)
    ts(DVE, hv[:, HCS:HCS + 16], hv[:, CS:CS + 16], -4.0, None, ALU.mult, None, [R_hv], [R_hv])
    ts(DVE, hv[:, CS:CS + 16], hv[:, CS:CS + 16], -8.0, None, ALU.mult, None, [R_hv], [R_hv])
    ts(DVE, hv[:, HBA:HBA + 16], vecs[:, V_BA:V_BA + 16], 0.5, None, ALU.mult, None, [R_vecs], [R_hv])
    ts(DVE, hv[:, HBX:HBX + 16], vecs[:, V_BX:V_BX + 16], 0.5, None, ALU.mult, None, [R_vecs], [R_hv])

    WM = {}

    def mod_setup(n):
        WM["t"] = [A.alloc("wm", [128, 8, 512], BF16) for _ in range(n)]
        WM["R"] = [Res("wm%d" % i) for i in range(n)]
        WM["i"] = 0
        WM["gtrow"] = A.alloc("gtrow", [1, 512], F32)
        WM["R_gtrow"] = Res("gtrow")
        WM["mtmp"] = A.alloc("mtmp", [128, 8], F32)
        WM["R_mtmp"] = Res("mtmp")
        WM["bmrow"] = A.alloc("bmrow", [1, 512], F32)
        WM["R_bmrow"] = Res("bmrow")

    def mod_load(b, engs):
        s_ = WM["i"] % len(WM["t"])
        WM["i"] += 1
        WM["cur"] = s_
        if engs is None:
            dma(POOL, WM["t"][s_][:], wmod_d[b], writes=[WM["R"][s_]], sem="wmsw%d" % s_)
        else:
            load_w(WM["t"][s_][:], wmod_d[b], 8, 512, WM["R"][s_], engs)

    def mod_block(b, engs):
        mod_load(b, engs)
        mod_compute(b)

    def mod_compute(b):
        s_ = WM["cur"]
        wm, R_wm = WM["t"][s_], WM["R"][s_]
        pm, R_pm = nbank()
        if b in (0, 1, 2, 3, 6, 7, 8, 9):
            for j in range(4):
                for k in range(8):
                    mm(pm[:, 2 * j:2 * j + 2], wm[:, k, j * 128:(j + 1) * 128], siluc[:, k:16:8],
                       k == 0, k == 7, [R_wm, R_siluc], [R_pm])
            c0 = b * 4 if b < 4 else 16 + (b - 6) * 4
            vb = V_BM + b * 4
            act(WM["mtmp"][:, 0:8], pm[:, 0:8], AF.Identity, [R_pm], [WM["R_mtmp"]])
            tt(DVE, modL[:, c0:c0 + 4], WM["mtmp"][:, 0:8:2], vecs[:, vb:vb + 4], ALU.add, [WM["R_mtmp"], R_vecs], [R_mod])
            if b < 4:
                tt(DVE, modC[:, c0:c0 + 4], WM["mtmp"][:, 1:8:2], vecs[:, vb:vb + 4], ALU.add, [WM["R_mtmp"], R_vecs], [R_mod])
        else:
            gi = {4: 0, 5: 1, 10: 2, 11: 3}[b]
            gtrow, R_gtrow, bmrow, R_bmrow = WM["gtrow"], WM["R_gtrow"], WM["bmrow"], WM["R_bmrow"]
            dma(SP, bmrow[:], bmod_d[:, b * 512:(b + 1) * 512], writes=[R_bmrow], sem="bmrow")
            for k in range(8):
                mm(pm[0:1, :], siluc[:, k:k + 1], wm[:, k, :], k == 0, k == 7, [R_wm, R_siluc], [R_pm])
            act(gtrow[:], pm[0:1, :], AF.Identity, [R_pm], [R_gtrow])
            tt(DVE, gtrow[:], gtrow[:], bmrow[:], ALU.add, [R_gtrow, R_bmrow], [R_gtrow])
            dma(SP, gt_d[:, gi * 512:(gi + 1) * 512], gtrow[:], reads=[R_gtrow], writes=[R_gtd], sem="gtd")

    def mod_finish2():
        stt(hv[:, GMOD2:GMOD2 + 8], modL[:, 24:32], 1.0, vecs[:, V_GMLP:V_GMLP + 8], ALU.add, ALU.mult, [R_mod, R_vecs], [R_hv])
        copy(DVE, hv[:, SH2:SH2 + 8], modL[:, 16:24], [R_mod], [R_hv])

    mA = A.mark()
    mod_setup(2)
    stg_setup(3)
    for b in range(4):
        mod_block(b, [DVE, ACT])
    stt(hv[:, GMOD1:GMOD1 + 8], modL[:, 8:16], 1.0, vecs[:, V_GMIX:V_GMIX + 8], ALU.add, ALU.mult, [R_mod, R_vecs], [R_hv])
    copy(DVE, hv[:, SH1:SH1 + 8], modL[:, 0:8], [R_mod], [R_hv])
    stt(hv[:, GMOD1C:GMOD1C + 8], modC[:, 8:16], 1.0, vecs[:, V_GMIX:V_GMIX + 8], ALU.add, ALU.mult, [R_mod, R_vecs], [R_hv])
    copy(DVE, hv[:, SH1C:SH1C + 8], modC[:, 0:8], [R_mod], [R_hv])

    def nt_stage1(tiles, R_tiles, ss, R_ss, rstd, sq, xs, R_xs, n, dve_share=True):
        for i in range(n):
            act(xs[:, i, :], tiles[i], AF.Square, [R_tiles[i]], [R_xs, R_ss], accum=ss[:, i:i + 1])
        act_fence([R_ss])
        act(sq[:, 0:n], ss[:, 0:n], AF.Sqrt, [R_ss, R_hv], [R_ss], scale=1.0 / D, bias=hv[:, EPSC:EPSC + 1])
        recip(rstd[:, 0:n], sq[:, 0:n], [R_ss], [R_ss])
        for i in range(n):
            if dve_share and i % 2 == 0:
                ts(DVE, xs[:, i, :], tiles[i], rstd[:, i:i + 1], None, ALU.mult, None, [R_tiles[i], R_ss], [R_xs])
            else:
                act(xs[:, i, :], tiles[i], AF.Identity, [R_tiles[i], R_ss], [R_xs], scale=rstd[:, i:i + 1])

    def nt_stage2(xs, R_xs, dstT, dst_off, R_dst, gcol, scol, n, dve_share=True):
        for k in range(8):
            pb, R_pb = nbank()
            pT = pb[:].bitcast(BF16)
            for i in range(n):
                tr(pT[:, i * 128:(i + 1) * 128], xs[:, i, k * 128:(k + 1) * 128], ident[:], [R_xs, R_ident], [R_pb])
            dst = dstT[:, k, dst_off:dst_off + n * 128]
            if (k % 4 != 3) if dve_share else (k % 2 == 0):
                ts(DVE, dst, pT[:, 0:n * 128], hv[:, gcol + k:gcol + k + 1], hv[:, scol + k:scol + k + 1],
                   ALU.mult, ALU.add, [R_pb, R_hv], [R_dst])
            else:
                act(dst, pT[:, 0:n * 128], AF.Identity, [R_pb, R_hv], [R_dst],
                    scale=hv[:, gcol + k:gcol + k + 1], bias=hv[:, scol + k:scol + k + 1])

    NPT, NXT = 3, 6
    pt = [A.alloc("pt", [128, D], F32) for _ in range(NPT)]
    R_pt = [Res("pt%d" % i) for i in range(NPT)]
    xt = [A.alloc("xt", [128, D], F32) for _ in range(NXT)]
    R_xt = [Res("xt%d" % i) for i in range(NXT)]
    TOPC = 34 * 1024
    ctab0 = A.alloc_top("ctab0", [128, 16, 512], BF16, 34 * 1024)
    stab0 = A.alloc_top("stab0", [128, 16, 512], BF16, 18 * 1024)
    wf0 = A.alloc_top("wf0", [128, 8, 128], BF16, 2 * 1024)
    R_ctab = [Res("ctab%d" % i) for i in range(2)]
    R_stab = [Res("stab%d" % i) for i in range(2)]
    R_wf = [Res("wf%d" % i) for i in range(2)]
    xs = [A.alloc("xs", [128, 4, D], BF16) for _ in range(2)]
    R_xs = [Res("xs%d" % i) for i in range(2)]
    ssA = [A.alloc("ssA", [128, 4], F32) for _ in range(2)]
    sqA = [A.alloc("sqA", [128, 4], F32) for _ in range(2)]
    rsA = [A.alloc("rsA", [128, 4], F32) for _ in range(2)]
    R_ssA = [Res("ssA%d" % i) for i in range(2)]
    R_xpd = [Res("xpd%d" % i) for i in range(16)]

    def A_s1(gi):
        sl = gi % 2
        tl, Rl = [], []
        if gi == 0:
            for i in range(2):
                dma(SP, xt[i][:], ctx_d[i * 128:(i + 1) * 128, :], writes=[R_xt[i]])
                tl.append(xt[i][:])
                Rl.append(R_xt[i])
        else:
            g = gi - 1
            for i in range(4):
                ti = g * 4 + i
                s_ = (2 + ti) % NXT
                dma(SP, xt[s_][:], x_d[ti * 128:(ti + 1) * 128, :], writes=[R_xt[s_]])
                dma(SP, pt[ti % NPT][:], pos_d[ti * 128:(ti + 1) * 128, :], writes=[R_pt[ti % NPT]])
                tt(DVE, xt[s_][:], xt[s_][:], pt[ti % NPT][:], ALU.add, [R_xt[s_], R_pt[ti % NPT]], [R_xt[s_]])
                tl.append(xt[s_][:])
                Rl.append(R_xt[s_])
        nt_stage1(tl, Rl, ssA[sl], R_ssA[sl], rsA[sl], sqA[sl], xs[sl], R_xs[sl], len(tl), dve_share=False)

    def A_s2(gi):
        sl = gi % 2
        if gi == 0:
            nt_stage2(xs[sl], R_xs[sl], hcT, 0, R_hcT, GMOD1C, SH1C, 2, dve_share=False)
        else:
            g = gi - 1
            nt_stage2(xs[sl], R_xs[sl], hT, g * 512, R_hT[g], GMOD1, SH1, 4, dve_share=False)

    A_s1(0)
    for gi in range(5):
        if gi + 1 < 5:
            A_s1(gi + 1)
        A_s2(gi)
        if gi == 2:
            load_w(wf0[:], win_d[16], 8, 128, R_wf[0], [DVE])
            dma(SP, ctab0[:], ct_d[0], writes=[R_ctab[0]])
            dma(SP, stab0[:], st_d[0], writes=[R_stab[0]])
    A.limit(TOPC)
    P.barrier()
    A.release(mA)

    if debug:
        finals.append(dma(SP, dbg["hT"], hT[:], reads=R_hT, sem="dbg"))
        finals.append(dma(SP, dbg["hv"], hv[:], reads=[R_hv], sem="dbg"))
        finals.append(dma(SP, dbg["siluc"], siluc[:], reads=[R_siluc], sem="dbg"))
        finals.append(dma(SP, dbg["modL"], modL[:], reads=[R_mod], sem="dbg"))
        finals.append(dma(SP, dbg["vecs"], vecs[:], reads=[R_vecs], sem="dbg"))
    P.barrier()

    mC = A.mark()
    Z = A.alloc("Z", [128, 16, 4, 256], BF16)
    R_Z = [Res("Z%d" % g) for g in range(4)]
    stg_setup(2)
    fcm = A.alloc("fcm", [128, 256], BF16)
    R_fcm = Res("fcm")
    dma(SP, fcm[:], fc_d, writes=[R_fcm], sem="const")
    wf = [wf0, A.alloc("wf", [128, 8, 128], BF16)]
    ufT1 = A.alloc("ufT", [128, T], BF16)
    ufT = [ufT1, ufT1]
    R_uf1 = Res("uf")
    R_uf = [R_uf1, R_uf1]
    ctab = [ctab0, A.alloc("ctab", [128, 16, 512], BF16)]
    stab = [stab0, A.alloc("stab", [128, 16, 512], BF16)]
    cnq = A.alloc("cnq", [128, 2], BF16)
    R_cnq = Res("cnq")
    dma(SP, cnq[:], cn_d, writes=[R_cnq], sem="const")
    psb = [A.alloc("psb", [128, 512], F32) for _ in range(2)]
    R_psb = [Res("psb%d" % i) for i in range(2)]
    dma(SP, ctab[1][:], ct_d[1], writes=[R_ctab[1]])
    dma(SP, stab[1][:], st_d[1], writes=[R_stab[1]])
    A.limit(TOPC)
    for g in range(4):
        s = g % 2
        if g + 1 < 4:
            load_w(wf[1 - s][:], win_d[16 + g + 1], 8, 128, R_wf[1 - s], [DVE])
        for blk in range(4):
            pb, R_pb = nbank()
            for k in range(8):
                mm(pb[:], wf[s][:, k, :], hT[:, k, blk * 512:(blk + 1) * 512], k == 0, k == 7,
                   [R_wf[s], R_hT[blk]], [R_pb])
            if blk % 2 == 0:
                act(ufT[s][:, blk * 512:(blk + 1) * 512], pb[:], AF.Identity, [R_pb], [R_uf[s]])
            else:
                copy(DVE, ufT[s][:, blk * 512:(blk + 1) * 512], pb[:], [R_pb], [R_uf[s]])
        for t2 in range(8):
            pb, R_pb = nbank()
            for i in range(2):
                ti = t2 * 2 + i
                mm(pb[:, i * 256:(i + 1) * 256], ufT[s][:, ti * 128:(ti + 1) * 128], fcm[:], True, True,
                   [R_uf[s], R_fcm], [R_pb])
            zin = pb[:].rearrange("p (a b) -> p a b", a=2)
            if t2 % 2 == 0:
                act(Z[:, t2 * 2:t2 * 2 + 2, g, :], zin, AF.Identity, [R_pb], [R_Z[g]])
            else:
                copy(DVE, Z[:, t2 * 2:t2 * 2 + 2, g, :], zin, [R_pb], [R_Z[g]])
    pnq, R_pnq = nbank()
    for g in range(4):
        for j in range(16):
            mm(pnq[:, 2 * g:2 * g + 2], Z[:, j, g, 0:128], cnq[:], j == 0, j == 15, [R_Z[g], R_cnq], [R_pnq])
    act(yfT[:, :, 1024:1025], pnq[:, 0:8:2].rearrange("p (g o) -> p g o", o=1), AF.Identity, [R_pnq], R_yf)
    YTOT = 4 * T
    for kt in range(2):
        for g in range(4):
            pP, R_pP = nbank()
            for j in range(16):
                mm(pP[:], Z[:, j, g, 0:128], ctab[kt][:, j, :], j == 0, j == 15, [R_Z[g], R_ctab[kt]], [R_pP])
            pQ, R_pQ = nbank()
            for j in range(16):
                mm(pQ[:], Z[:, j, g, 128:256], stab[kt][:, j, :], j == 0, j == 15, [R_Z[g], R_stab[kt]], [R_pQ])
            u = (kt * 4 + g) % 2
            act(psb[u][:], pP[:], AF.Identity, [R_pP], [R_psb[u]])
            tt(DVE, yfT[:, g, kt * 512:(kt + 1) * 512], pQ[:], psb[u][:], ALU.add, [R_pQ, R_psb[u]], [R_yf[g]])
            if kt == 0:
                outm = bass.AP(yfT, g * T + (T - 1), [[YTOT, 128], [-1, 511]])
                tt(DVE, outm, psb[u][:, 1:512], pQ[:, 1:512], ALU.subtract, [R_pQ, R_psb[u]], [R_yf[g]])
            else:
                outm = bass.AP(yfT, g * T + 1536, [[YTOT, 128], [-1, 512]])
                tt(DVE, outm, psb[u][:], pQ[:], ALU.subtract, [R_pQ, R_psb[u]], [R_yf[g]])
    if debug:
        finals.append(dma(SP, dbg["yfT"], yfT[:], reads=R_yf, sem="dbg"))
    P.barrier()
    A.limit(TOPC)
    A.release(mC)

    mB = A.mark()
    TL = CT + T
    XL = TL + CT
    wx = [A.alloc("wx", [128, 8, 128], BF16) for _ in range(2)]
    wy = [A.alloc("wy", [128, 8, 128], BF16) for _ in range(2)]
    wgt = [A.alloc("wgt", [128, 4, 128], BF16) for _ in range(2)]
    dg = [A.alloc("dg", [128, 4, 128], BF16) for _ in range(2)]
    R_wx = [Res("wx%d" % i) for i in range(2)]
    R_wy = [Res("wy%d" % i) for i in range(2)]
    R_wgt = [Res("wgt%d" % i) for i in range(2)]
    R_dg = [Res("dg%d" % i) for i in range(2)]
    uxp = A.alloc("uxp", [128, T + 4], BF16)
    uxcp = A.alloc("uxcp", [128, CT + 4], BF16)
    R_uxp = [Res("uxp%d" % i) for i in range(5)]
    xc = [A.alloc("xc", [128, XL], BF16) for _ in range(2)]
    R_xc = [[Res("xc%d_%d" % (i, b)) for b in range(5)] for i in range(2)]
    thi = [A.alloc("thi", [128, TL], BF16) for _ in range(2)]
    aa = [A.alloc("aa", [128, TL], F32) for _ in range(2)]
    thr = [A.alloc("thr", [128, TL], F32) for _ in range(2)]
    mult = [A.alloc("mult", [128, TL], BF16) for _ in range(2)]
    R_thi = [[Res("thi%d_%d" % (i, b)) for b in range(5)] for i in range(2)]
    R_thr = [[Res("thr%d_%d" % (d, b)) for b in range(5)] for d in range(2)]
    R_aa = [Res("aa%d" % i) for i in range(2)]
    R_mult = [Res("mult%d" % i) for i in range(2)]
    t1b = A.alloc("t1b", [128, TL], BF16)
    R_t1 = Res("t1b")
    hfb = [A.alloc("hfb", [128, TL], BF16) for _ in range(2)]
    R_hfb = [Res("hfb%d" % d) for d in range(2)]
    gl = A.alloc("gl", [128, T], BF16)
    R_gl = Res("gl")

    stg_setup(3)
    MODB = [6, 7, 8, 9, 4, 5, 10, 11]
    memset(DVE, uxp[:, 0:2], 0.0, [R_uxp[1]])
    memset(DVE, uxp[:, T + 2:T + 4], 0.0, [R_uxp[4]])
    memset(DVE, uxcp[:, 0:2], 0.0, [R_uxp[0]])
    memset(DVE, uxcp[:, CT + 2:CT + 4], 0.0, [R_uxp[0]])

    XOFF = [0] + [CT + b * 512 for b in range(4)]
    BLEN = [CT, 512, 512, 512, 512]
    DOFF = [[0] + [CT + b * 512 for b in range(4)], [T] + [b * 512 for b in range(4)]]

    def load_head_weights(h):
        s = h % 2
        load_w(wx[s][:], win_d[h], 8, 128, R_wx[s], [DVE])
        load_w(wy[s][:], win_d[8 + h], 8, 128, R_wy[s], [DVE])
        load_w(wgt[s][:], wg_d[h], 4, 128, R_wgt[s], [DVE])

    def front(h):
        s = h % 2
        for k in range(4):
            ts(DVE, dg[s][:, k, :], identf[:], vecs[:, V_CW + k * 8 + h:V_CW + k * 8 + h + 1], None, ALU.mult, None,
               [R_identf, R_vecs], [R_dg[s]])
        for b in range(5):
            n = BLEN[b]
            pb, R_pb = nbank()
            for k in range(8):
                rhs = hcT[:, k, :] if b == 0 else hT[:, k, (b - 1) * 512:b * 512]
                mm(pb[:, 0:n], wx[s][:, k, :], rhs, k == 0, k == 7,
                   [R_wx[s], R_hcT if b == 0 else R_hT[b - 1]], [R_pb])
            dst = uxcp[:, 2:2 + CT] if b == 0 else uxp[:, 2 + (b - 1) * 512:2 + b * 512]
            copy(DVE, dst, pb[:, 0:n], [R_pb], [R_uxp[b]])
        for b in range(5):
            n = BLEN[b]
            pb, R_pb = nbank()
            for k in range(4):
                if b == 0:
                    rhs = uxcp[:, k:k + CT]
                    rr = [R_uxp[0]]
                else:
                    st0 = (b - 1) * 512 + k
                    rhs = uxp[:, st0:st0 + 512]
                    rr = [R_uxp[j] for j in range(max(1, b - 1), min(4, b + 1) + 1)]
                mm(pb[:, 0:n], dg[s][:, k, :], rhs, k == 0, k == 3, [R_dg[s]] + rr, [R_pb])
            ts(DVE, xc[s][:, XOFF[b]:XOFF[b] + n], pb[:, 0:n], vecs[:, V_CB + h:V_CB + h + 1], None, ALU.add, None,
               [R_pb, R_vecs], [R_xc[s][b]])
            if b == 0:
                ts(DVE, xc[s][:, TL:TL + n], pb[:, 0:n], vecs[:, V_CB + h:V_CB + h + 1], None, ALU.add, None,
                   [R_pb, R_vecs], [R_xc[s][b]])

    def gates_te(h, d):
        s = h % 2
        q = d
        col = d * 8 + h
        for b in range(5):
            n = BLEN[b]
            xo, do = XOFF[b], DOFF[d][b]
            pr, R_pr = nbank()
            mm(pr[:, 0:n], wgt[s][:, d, :], xc[s][:, xo:xo + n], True, True, [R_wgt[s], R_xc[s][b]], [R_pr])
            pi, R_pi = nbank()
            mm(pi[:, 0:n], wgt[s][:, 2 + d, :], xc[s][:, xo:xo + n], True, True, [R_wgt[s], R_xc[s][b]], [R_pi])
            act(thr[d][:, do:do + n], pr[:, 0:n], AF.Tanh, [R_pr, R_hv], [R_thr[d][b]], scale=0.5,
                bias=hv[:, HBA + col:HBA + col + 1])
            act(thi[q][:, do:do + n], pi[:, 0:n], AF.Tanh, [R_pi, R_hv], [R_thi[q][b]], scale=0.5,
                bias=hv[:, HBX + col:HBX + col + 1])
        act(aa[q][:], thr[d][:], AF.Exp, R_thr[d] + [R_hv], [R_aa[q]],
            scale=hv[:, HCS + col:HCS + col + 1], bias=hv[:, HCS + col:HCS + col + 1])
        act(thr[d][:], thr[d][:], AF.Exp, R_thr[d] + [R_hv], R_thr[d],
            scale=hv[:, CS + col:CS + col + 1], bias=hv[:, CS + col:CS + col + 1])

    def gates_sqrt(h, d):
        q = d
        act(mult[q][:], thr[d][:], AF.Sqrt, R_thr[d] + [R_hv], [R_mult[q]],
            scale=hv[:, NQ:NQ + 1], bias=hv[:, QQ:QQ + 1])

    def dve_scan(h, d):
        s = h % 2
        q = d
        xview = xc[s][:, 0:TL] if d == 0 else xc[s][:, CT:XL]
        stt(t1b[:], thi[q][:], 1.0, xview, ALU.add, ALU.mult, R_thi[q] + R_xc[s], [R_t1])
        tt(DVE, t1b[:], t1b[:], mult[q][:], ALU.mult, [R_t1, R_mult[q]], [R_t1])
        if d == 0:
            scan(hfb[d][:], aa[q][:], t1b[:], 0.0, [R_aa[q], R_t1], [R_hfb[d]])
        else:
            scan(rev(hfb[d], 0, TL, TL), rev(aa[q], 0, TL, TL), rev(t1b, 0, TL, TL), 0.0, [R_aa[q], R_t1], [R_hfb[d]])

    def rev(tn, o0, nn, tot):
        return bass.AP(tn, o0 + nn - 1, [[tot, 128], [-1, nn]])

    def uy_gelu(h):
        s = h % 2
        for b in range(4):
            pb, R_pb = nbank()
            for k in range(8):
                mm(pb[:], wy[s][:, k, :], hT[:, k, b * 512:(b + 1) * 512], k == 0, k == 7, [R_wy[s], R_hT[b]], [R_pb])
            act(gl[:, b * 512:(b + 1) * 512], pb[:], AF.Gelu_apprx_tanh, [R_pb], [R_gl])

    def finish(h):
        tt(DVE, hfb[0][:, CT:TL], hfb[0][:, CT:TL], hfb[1][:, 0:T], ALU.add, R_hfb, [R_hfb[0]])
        tt(DVE, ylruT[:, h, :], hfb[0][:, CT:TL], gl[:], ALU.mult, [R_hfb[0], R_gl], [R_ylru[h]])

    TOPD = 6 * 1024
    wug0 = A.alloc_top("wug0", [128, 2, 8, 128], BF16, 6 * 1024)
    wlr0 = A.alloc_top("wlr0", [128, 8, 128], BF16, 2 * 1024)
    R_wug = [Res("wug%d" % i) for i in range(2)]
    R_wlr = [Res("wlr%d" % i) for i in range(2)]
    A.limit(TOPD)
    load_head_weights(0)
    front(0)
    for h in range(8):
        if h + 1 < 8:
            load_head_weights(h + 1)
        else:
            load_w(wug0[:, 0, :, :], win_d[20], 8, 128, R_wug[0], [DVE])
            load_w(wug0[:, 1, :, :], win_d[28], 8, 128, R_wug[0], [DVE])
            load_w(wlr0[:], wlru_d[0], 8, 128, R_wlr[0], [DVE])
        gates_te(h, 0)
        gates_te(h, 1)
        if h + 1 < 8:
            front(h + 1)
        gates_sqrt(h, 0)
        gates_sqrt(h, 1)
        uy_gelu(h)
        dve_scan(h, 0)
        dve_scan(h, 1)
        finish(h)
    if debug:
        finals.append(dma(SP, dbg["ylruT"], ylruT[:], reads=R_ylru, sem="dbg"))
    P.barrier()
    A.release(mB)

    mD = A.mark()
    mergedT = A.alloc("mergedT", [128, 8, T], BF16)
    R_mg = [Res("mg%d" % b) for b in range(4)]
    woutb = A.alloc("woutb", [128, 8, D], BF16)
    R_wout = Res("wout")
    stg_setup(3)
    wug = [wug0, A.alloc("wug", [128, 2, 8, 128], BF16)]
    OFF_WUG = A.last_off
    wlr = [wlr0, A.alloc("wlr", [128, 8, 128], BF16)]
    wfb = [A.alloc("wfb", [128, 4, 128], BF16) for _ in range(2)]
    R_wfb = [Res("wfb%d" % i) for i in range(2)]
    g1b = [A.alloc("g1b", [128, 512], BF16) for _ in range(2)]
    g2b = [A.alloc("g2b", [128, 512], BF16) for _ in range(2)]
    t1m = [A.alloc("t1m", [128, 512], BF16) for _ in range(2)]
    t2m = [A.alloc("t2m", [128, 512], BF16) for _ in range(2)]
    R_g1 = [Res("g1b%d" % i) for i in range(2)]
    R_g2 = [Res("g2b%d" % i) for i in range(2)]
    R_t1m = [Res("t1m%d" % i) for i in range(2)]
    R_t2m = [Res("t2m%d" % i) for i in range(2)]
    gt1bc = A.alloc("gt1bc", [128, D], F32)
    R_gt1bc = Res("gt1bc")
    xpt = [A.alloc("xpt", [128, D], F32) for _ in range(4)]
    R_xpt = [Res("xpt%d" % i) for i in range(4)]
    tmpd = [A.alloc("tmpd", [128, 512], F32) for _ in range(2)]
    R_tmpd = [Res("tmpd%d" % i) for i in range(2)]
    R_x1d = [Res("x1d%d" % i) for i in range(16)]

    mod_setup(1)

    def load_dc_weights(dc):
        s = dc % 2
        if dc > 0:
            load_w(wug[s][:, 0, :, :], win_d[20 + dc], 8, 128, R_wug[s], [ACT, DVE])
            load_w(wug[s][:, 1, :, :], win_d[28 + dc], 8, 128, R_wug[s], [ACT, DVE])
            load_w(wlr[s][:], wlru_d[dc], 8, 128, R_wlr[s], [ACT, DVE])
        load_w(wfb[s][:], wfo_d[dc], 4, 128, R_wfb[s], [ACT, DVE])

    def load_wout_piece(k):
        sl = STG["i"] % len(STG["tiles"])
        STG["i"] += 1
        sv = STG["tiles"][sl][:, 0:D]
        dma(SP, sv, wout_d[:, k, :], writes=[STG["R"][sl]], sem="stg%d" % sl)
        tt(DVE, woutb[:, k, :], sv, gt1bc[:], ALU.mult, [STG["R"][sl], R_gt1bc], [R_wout])

    MODD = [4, 5, 6, 7, 8, 9, 10, 11]
    load_dc_weights(0)
    ci = 0
    for dc in range(8):
        s = dc % 2
        if dc + 1 < 8:
            load_dc_weights(dc + 1)
        mod_load(MODD[dc], None)
        if dc == 2:
            dma(SP, gt1bc[:], gt_d[:, 0:1024].broadcast_to([128, D]), reads=[R_gtd], writes=[R_gt1bc], sem="const2")
        if 3 <= dc <= 6:
            load_wout_piece(2 * (dc - 3))
            load_wout_piece(2 * (dc - 3) + 1)
        for b in range(4):
            u = ci % 2
            ci += 1
            tsl = slice(b * 512, (b + 1) * 512)
            p1, R_p1 = nbank()
            for k in range(8):
                mm(p1[:], wug[s][:, 0, k, :], hT[:, k, tsl], k == 0, k == 7, [R_wug[s], R_hT[b]], [R_p1])
            p2, R_p2 = nbank()
            for k in range(8):
                mm(p2[:], wug[s][:, 1, k, :], hT[:, k, tsl], k == 0, k == 7, [R_wug[s], R_hT[b]], [R_p2])
            p3, R_p3 = nbank()
            for k in range(8):
                mm(p3[:], wlr[s][:, k, :], ylruT[:, k, tsl], k == 0, k == 7, [R_wlr[s], R_ylru[k]], [R_p3])
            p4, R_p4 = nbank()
            for k in range(4):
                mm(p4[:], wfb[s][:, k, :], yfT[:, k, tsl], k == 0, k == 3, [R_wfb[s], R_yf[k]], [R_p4])
            act(g1b[u][:], p1[:], AF.Sigmoid, [R_p1], [R_g1[u]])
            act(g2b[u][:], p2[:], AF.Sigmoid, [R_p2], [R_g2[u]])
            tt(DVE, t1m[u][:], p3[:], g1b[u][:], ALU.mult, [R_p3, R_g1[u]], [R_t1m[u]])
            tt(DVE, t2m[u][:], p4[:], g2b[u][:], ALU.mult, [R_p4, R_g2[u]], [R_t2m[u]])
            tt(DVE, mergedT[:, dc, tsl], t1m[u][:], t2m[u][:], ALU.add, [R_t1m[u], R_t2m[u]], [R_mg[b]])
        mod_compute(MODD[dc])
        if dc == 5:
            mod_finish2()
    if debug:
        finals.append(dma(SP, dbg["mergedT"], mergedT[:], reads=R_mg, sem="dbg"))
    P.barrier()
    ptd = [A.alloc_abs("ptd", [128, 2, D], BF16, OFF_WUG + i * 4096) for i in range(4)]
    R_ptd = [Res("ptd%d" % i) for i in range(4)]
    xs2all = A.alloc_abs("xs2all", [128, 16, D], BF16, OFF_YLRU)
    R_xsa = [Res("xsa%d" % i) for i in range(4)]
    ssd = A.alloc("ssd", [128, 16], F32)
    sqd = A.alloc("sqd", [128, 16], F32)
    rsd = A.alloc("rsd", [128, 16], F32)
    R_ssd = [Res("ssd%d" % i) for i in range(4)]

    def D_s0(ti):
        s8 = ti % 4
        dma(SP, xpt[s8][:], x_d[ti * 128:(ti + 1) * 128, :], writes=[R_xpt[s8]])
        dma(SP, ptd[s8][:], pos2_d[ti * 128:(ti + 1) * 128, :, :], writes=[R_ptd[s8]])

    def D_s1(ti):
        s8 = ti % 4
        g4 = ti // 4
        for hh in range(2):
            pb, R_pb = nbank()
            for k in range(8):
                mm(pb[:], mergedT[:, k, ti * 128:(ti + 1) * 128], woutb[:, k, hh * 512:(hh + 1) * 512],
                   k == 0, False, [R_mg[ti // 4], R_wout], [R_pb])
            mm(pb[:], ident[:], ptd[s8][:, 0, hh * 512:(hh + 1) * 512], False, False, [R_ident, R_ptd[s8]], [R_pb])
            mm(pb[:], ident[:], ptd[s8][:, 1, hh * 512:(hh + 1) * 512], False, True, [R_ident, R_ptd[s8]], [R_pb])
            tt(DVE, xpt[s8][:, hh * 512:(hh + 1) * 512], pb[:], xpt[s8][:, hh * 512:(hh + 1) * 512], ALU.add,
               [R_pb, R_xpt[s8]], [R_xpt[s8]])
        dma(POOL, x1_d[ti * 128:(ti + 1) * 128, :], xpt[s8][:], reads=[R_xpt[s8]], writes=[R_x1d[ti]], sem="x1d%d" % s8)
        act(xs2all[:, ti, :], xpt[s8][:], AF.Square, [R_xpt[s8]], [R_xsa[g4], R_ssd[g4]], accum=ssd[:, ti:ti + 1])
        act_fence([R_ssd[g4]])
        act(sqd[:, ti:ti + 1], ssd[:, ti:ti + 1], AF.Sqrt, [R_ssd[g4], R_hv], [R_ssd[g4]], scale=1.0 / D, bias=hv[:, EPSC:EPSC + 1])
        recip(rsd[:, ti:ti + 1], sqd[:, ti:ti + 1], [R_ssd[g4]], [R_ssd[g4]])
        if ti % 2 == 0:
            ts(DVE, xs2all[:, ti, :], xpt[s8][:], rsd[:, ti:ti + 1], None, ALU.mult, None, [R_xpt[s8], R_ssd[g4]], [R_xsa[g4]])
        else:
            act(xs2all[:, ti, :], xpt[s8][:], AF.Identity, [R_xpt[s8], R_ssd[g4]], [R_xsa[g4]], scale=rsd[:, ti:ti + 1])

    D_s0(0)
    D_s0(1)
    for ti in range(16):
        if ti + 2 < 16:
            D_s0(ti + 2)
        D_s1(ti)
    w1t_top = [A.alloc_top("w1t", [128, 8, 128], BF16, (3 - i) * 2048) for i in range(3)]
    R_w1t = [Res("w1t%d" % i) for i in range(4)]
    for i in range(2):
        load_w(w1t_top[i][:], w1_d[i], 8, 128, R_w1t[i], [DVE, ACT])
    for g4 in range(4):
        nt_stage2(xs2all[:, g4 * 4:(g4 + 1) * 4, :], R_xsa[g4], hT, g4 * 512, R_hT[g4], GMOD2, SH2, 4)
    A.limit(TOPD)
    if debug:
        finals.append(dma(SP, dbg["h2T"], hT[:], reads=R_hT, sem="dbg"))
    P.barrier()
    A.release(mD)

    A.release(MARK_E)
    stg_setup(4)
    hid = A.alloc("hid", [128, 32, 1024], BF16)
    R_hid = [Res("hid%d" % i) for i in range(8)]
    x1y = A.alloc("x1y", [128, 8, D], F32)
    R_x1y = [Res("x1y%d" % j) for j in range(8)]
    w1t = w1t_top + [A.alloc("w1t", [128, 8, 128], BF16)]
    w2t = [A.alloc("w2t", [128, 4, 512], BF16) for _ in range(3)]
    R_w2t = [Res("w2t%d" % i) for i in range(3)]
    gt2bc = A.alloc("gt2bc", [128, D], F32)
    R_gt2bc = Res("gt2bc")
    gfbc = A.alloc("gfbc", [128, D], F32)
    R_gfbc = Res("gfbc")
    rl = [A.alloc("rl", [128, 512], BF16) for _ in range(3)]
    R_rl = [Res("rl%d" % i) for i in range(3)]
    tmpe = [A.alloc("tmpe", [128, 512], F32) for _ in range(8)]
    R_tmpe = [Res("tmpe%d" % i) for i in range(8)]
    junk3 = A.alloc("junk3", [128, 512], BF16)
    R_junk3 = Res("junk3")
    ssf = A.alloc("ssf", [128, 16], F32)
    R_ssf = Res("ssf")
    sqf = A.alloc("sqf", [128, 8], F32)
    rsf = A.alloc("rsf", [128, 8], F32)

    dma(SP, gt2bc[:], gt_d[:, 1024:2048].broadcast_to([128, D]), reads=[R_gtd], writes=[R_gt2bc], sem="const2")
    dma(SP, gfbc[:], gfin_d.broadcast_to([128, D]), writes=[R_gfbc], sem="const2")

    w1i = 0
    w2i = 0
    rli = 0
    tei = 0
    for tb in range(2):
        t0 = tb * 1024
        def ld1(fcn):
            load_w(w1t[fcn % 4][:], w1_d[fcn], 8, 128, R_w1t[fcn % 4], [DVE, ACT])

        def ld2(i):
            load_w(w2t[i % 3][:], w2_d[i // 8, i % 8], 4, 512, R_w2t[i % 3], [DVE, ACT])
        for fcn in range(32):
            s = fcn % 4
            if fcn + 2 < 32:
                ld1(fcn + 2)
            if fcn == 12:
                for j in range(8):
                    ti = tb * 8 + j
                    dma(SP, x1y[:, j, :], x1_d[ti * 128:(ti + 1) * 128, :], reads=[R_x1d[ti]], writes=[R_x1y[j]])
            if fcn == 28:
                ld2(0)
                ld2(1)
            for nb in range(2):
                pb, R_pb = nbank()
                for k in range(8):
                    mm(pb[:], w1t[s][:, k, :], hT[:, k, t0 + nb * 512:t0 + (nb + 1) * 512], k == 0, k == 7,
                       [R_w1t[s], R_hT[tb * 2 + nb]], [R_pb])
                r = rli % 3
                rli += 1
                act(rl[r][:], pb[:], AF.Relu, [R_pb], [R_rl[r]])
                tt(DVE, hid[:, fcn, nb * 512:(nb + 1) * 512], rl[r][:], rl[r][:], ALU.mult,
                   [R_rl[r]], [R_hid[fcn // 4]])
        for dh in range(2):
            dsl = slice(dh * 512, (dh + 1) * 512)
            for f4 in range(8):
                i2 = dh * 8 + f4
                s = i2 % 3
                if i2 + 2 < 16:
                    ld2(i2 + 2)
                for j in range(8):
                    for c in range(4):
                        fcn = f4 * 4 + c
                        mm(banks[j][:], hid[:, fcn, j * 128:(j + 1) * 128], w2t[s][:, c, :], fcn == 0, fcn == 31,
                           [R_hid[f4], R_w2t[s]], [R_bank[j]])
                if dh == 1 and f4 == 6 and tb == 0:
                    ld1(0)
                    ld1(1)
            for j in range(8):
                tt(DVE, tmpe[j][:], banks[j][:], gt2bc[:, dsl], ALU.mult, [R_bank[j], R_gt2bc], [R_tmpe[j]])
            for j in range(8):
                tt(DVE, x1y[:, j, dsl], tmpe[j][:], x1y[:, j, dsl], ALU.add,
                   [R_tmpe[j], R_x1y[j]], [R_x1y[j]])
                act(junk3[:], x1y[:, j, dsl], AF.Square, [R_x1y[j]], [R_junk3, R_ssf], accum=ssf[:, dh * 8 + j:dh * 8 + j + 1])
            act_fence([R_ssf])
        tt(DVE, ssf[:, 0:8], ssf[:, 0:8], ssf[:, 8:16], ALU.add, [R_ssf], [R_ssf])
        act(sqf[:], ssf[:, 0:8], AF.Sqrt, [R_ssf, R_hv], [R_ssf], scale=1.0 / D, bias=hv[:, EPSC:EPSC + 1])
        recip(rsf[:], sqf[:], [R_ssf], [R_ssf])
        for j in range(8):
            ti = tb * 8 + j
            for hh in range(2):
                hs = slice(hh * 512, (hh + 1) * 512)
                stt(x1y[:, j, hs], x1y[:, j, hs], rsf[:, j:j + 1], gfbc[:, hs], ALU.mult, ALU.mult,
                    [R_x1y[j], R_ssf, R_gfbc], [R_x1y[j]])
            finals.append(dma(POOL, out_d[ti * 128:(ti + 1) * 128, :], x1y[:, j, :], reads=[R_x1y[j]], sem="outst%d" % j))

    P.emit(finals)
    print("[build] ops:", {e: len(v) for e, v in P.streams.items()}, "dma sems:", len(P.dma_sems),
          "sbuf peak KiB:", (A.peak - A.base) / 1024.0)
    return nc


def _pos_table():
    def sincos(n, dim):
        half = dim // 2
        freqs = np.exp(-math.log(10000.0) * np.arange(half, dtype=np.float32) / half).astype(np.float32)
        ang = np.arange(n, dtype=np.float32)[:, None] * freqs[None, :]
        return np.concatenate([np.sin(ang), np.cos(ang)], axis=-1).astype(np.float32)
    rows = T // 64
    er = sincos(rows, D // 2)
    ec = sincos(64, D // 2)
    emb = np.concatenate([np.broadcast_to(er[:, None, :], (rows, 64, D // 2)),
                          np.broadcast_to(ec[None, :, :], (rows, 64, D // 2))], axis=-1)
    return np.ascontiguousarray(emb.reshape(T, D).astype(np.float32))


def _dft_tables():
    bf = ml_dtypes.bfloat16
    n = np.arange(128)
    angc = 2 * np.pi * np.outer(n, n) / 128.0
    fc = np.concatenate([np.cos(angc), -np.sin(angc)], axis=1) / math.sqrt(128.0)
    t = np.arange(T)
    angt = 2 * np.pi * (np.outer(t, t) % T) / T
    ctab = (np.cos(angt) / math.sqrt(T)).astype(np.float32)
    stab = (np.sin(angt) / math.sqrt(T)).astype(np.float32)

    def lay(m):
        return np.ascontiguousarray(m.reshape(16, 128, 4, 512).transpose(2, 1, 0, 3)[0:2]).astype(bf)
    cn = np.zeros((128, 2), np.float32)
    cn[:, 0] = ((-1.0) ** np.arange(128)) / math.sqrt(T)
    return fc.astype(np.float32).astype(bf), lay(ctab), lay(stab), cn.astype(bf)


def _kp(w, nblk, bw):
    K, N = w.shape
    return np.ascontiguousarray(w.reshape(K // 128, 128, nblk, bw).transpose(2, 1, 0, 3))


def _fm(v):
    return np.ascontiguousarray(np.asarray(v, np.float32).reshape(-1, 128).T)


_CACHE = {}


def kernel(x, c, ctx, c_ctx, w_mod, b_mod, g_mix, w_in, conv_w, conv_b, w_a, b_a, w_x, b_x,
           lam, w_lru_out, w_f_out, w_out, g_mlp, w1, w2, g_final):
    f = lambda a: np.asarray(a, np.float32)
    x, c, ctx, c_ctx = f(x), f(c), f(ctx), f(c_ctx)
    w_mod, b_mod, g_mix, w_in = f(w_mod)[0], f(b_mod)[0], f(g_mix)[0], f(w_in)[0]
    conv_w, conv_b, w_a, b_a, w_x, b_x, lam = f(conv_w)[0], f(conv_b)[0], f(w_a)[0], f(b_a)[0], f(w_x)[0], f(b_x)[0], f(lam)[0]
    w_lru_out, w_f_out, w_out, g_mlp, w1, w2, g_final = f(w_lru_out)[0], f(w_f_out)[0], f(w_out)[0], f(g_mlp)[0], f(w1)[0], f(w2)[0], f(g_final)
    B = x.shape[0]

    if "consts" not in _CACHE:
        fc, ctab, stab, cnyq = _dft_tables()
        pos = _pos_table()
        bf = ml_dtypes.bfloat16
        pos_hi = pos.astype(bf)
        pos_lo = (pos - pos_hi.astype(np.float32)).astype(bf)
        pos2 = np.ascontiguousarray(np.stack([pos_hi, pos_lo], axis=1))
        _CACHE["consts"] = dict(pos=pos, pos2=pos2, fc=fc, ctab=ctab, stab=stab, cnyq=cnyq,
                                ident=np.eye(128, dtype=np.float32).astype(ml_dtypes.bfloat16),
                                identf=np.eye(128, dtype=np.float32))
    cst = _CACHE["consts"]

    shared = dict(cst)
    shared["bmod"] = np.ascontiguousarray(b_mod.reshape(1, -1))
    shared["gfin"] = np.ascontiguousarray(g_final.reshape(1, -1))
    shared["wmod"] = _kp(w_mod, 12, 512)
    shared["win"] = _kp(w_in, 36, 128)
    wg = np.stack([w_a[0], w_a[1], w_x[0], w_x[1]], axis=0)
    shared["wg"] = np.ascontiguousarray(wg.transpose(1, 2, 0, 3))
    shared["wlru"] = _kp(w_lru_out, 8, 128)
    shared["wfo"] = _kp(w_f_out, 8, 128)
    shared["wout"] = np.ascontiguousarray(w_out.reshape(8, 128, D).transpose(1, 0, 2))
    shared["w1"] = _kp(w1, 32, 128)
    shared["w2"] = np.ascontiguousarray(w2.reshape(8, 4, 128, 2, 512).transpose(3, 0, 2, 1, 4))

    common_cols = [_fm(c_ctx), _fm(b_mod), _fm(g_mix), _fm(g_mlp),
                   np.concatenate([_fm(conv_w[k]) for k in range(4)], axis=1), _fm(conv_b),
                   np.concatenate([_fm(b_a[d]) for d in range(2)], axis=1),
                   np.concatenate([_fm(b_x[d]) for d in range(2)], axis=1),
                   np.concatenate([_fm(lam[d]) for d in range(2)], axis=1)]
    in_maps = []
    for b in range(B):
        m = dict(shared)
        m["x"] = np.ascontiguousarray(x[b])
        m["ctx"] = np.ascontiguousarray(ctx[b])
        m["vecs"] = np.ascontiguousarray(np.concatenate([_fm(c[b])] + common_cols, axis=1))
        assert m["vecs"].shape == (128, NV)
        in_maps.append(m)

    if "nc" not in _CACHE:
        _CACHE["nc"] = build_program(DEBUG)
    nc = _CACHE["nc"]
    res = run_bass_kernel_spmd(nc, in_maps, core_ids=list(range(B)))
    _CACHE["last"] = res
    out = np.stack([np.asarray(r["out"], np.float32) for r in res.results], axis=0)
    return out
```

```python
import math
import numpy as np
import ml_dtypes
import concourse.bass as bass
import concourse.mybir as mybir
from concourse.bass_utils import run_bass_kernel_spmd

F32 = mybir.dt.float32
BF16 = mybir.dt.bfloat16
AF = mybir.ActivationFunctionType
ALU = mybir.AluOpType

PE, ACT, DVE, POOL, SP = "tensor", "scalar", "vector", "gpsimd", "sync"
COMPUTE = (PE, ACT, DVE, POOL)

T = 2048
D = 1024
CT = 256
EPS = 1e-6
NV = 168
DEBUG = False


class Res:
    __slots__ = ("name", "writer", "readers")

    def __init__(self, name):
        self.name = name
        self.writer = None
        self.readers = []


class Op:
    __slots__ = ("eng", "fn", "deps", "rawdeps", "signaled", "sigval", "is_dma", "dsem")

    def __init__(self, eng, fn, is_dma=False):
        self.eng = eng
        self.fn = fn
        self.deps = []
        self.rawdeps = []
        self.signaled = False
        self.sigval = None
        self.is_dma = is_dma
        self.dsem = None


class Prog:
    def __init__(self, nc):
        self.nc = nc
        self.streams = {e: [] for e in (PE, ACT, DVE, POOL, SP)}
        self.dma_sems = {}
        self.last = {e: None for e in COMPUTE}

    def op(self, eng, fn, reads=(), writes=(), dma=False, dkey=None):
        o = Op(eng, fn, is_dma=dma)
        deps = []
        for r in reads:
            if r.writer is not None:
                deps.append(r.writer)
        for w in writes:
            if w.writer is not None:
                deps.append(w.writer)
            deps.extend(w.readers)
        seen = set()
        for d in deps:
            if id(d) in seen or d is o:
                continue
            seen.add(id(d))
            if (not d.is_dma) and d.eng == PE and eng == PE and not dma:
                continue
            if d.is_dma:
                o.rawdeps.append((d.dsem, 16 * d.dsem[1]))
                continue
            o.deps.append(d)
            d.signaled = True
        if dma:
            if dkey is None:
                key = writes[0] if writes else reads[0]
                kid, kname = id(key), key.name
            else:
                kid, kname = dkey, dkey
            ent = self.dma_sems.get(kid)
            if ent is None:
                ent = [None, 0, kname]
                self.dma_sems[kid] = ent
            ent[1] += 1
            o.dsem = ent
            o.sigval = 16 * ent[1]
        else:
            self.last[eng] = o
        for r in reads:
            r.readers.append(o)
        for w in writes:
            w.writer = o
            w.readers = []
        self.streams[eng].append(o)
        return o

    def barrier(self):
        tails = [o for o in self.last.values() if o is not None]
        for o in tails:
            o.signaled = True
        dstate = [(ent, 16 * ent[1]) for ent in self.dma_sems.values()]
        for e in (PE, ACT, DVE, POOL, SP):
            b = Op(e, None)
            b.deps = [t for t in tails if t.eng != e]
            b.rawdeps = list(dstate)
            self.streams[e].append(b)

    def emit(self, final_ops):
        nc = self.nc
        from contextlib import ExitStack
        for e in COMPUTE:
            cnt = 0
            for o in self.streams[e]:
                if o.is_dma or o.fn is None:
                    continue
                if o.signaled:
                    cnt += 1
                    o.sigval = cnt
        with ExitStack() as es:
            tsem = {e: es.enter_context(nc.semaphore("tl_" + e)) for e in COMPUTE}
            for i, ent in enumerate(self.dma_sems.values()):
                ent[0] = es.enter_context(nc.semaphore("d%d_%s" % (i, ent[2])))
            block = es.enter_context(nc.Block())

            def run_stream(e, engobj):
                waited = {}

                def w(sem, val):
                    k = id(sem)
                    if waited.get(k, 0) >= val:
                        return
                    waited[k] = val
                    engobj.wait_ge(sem, val)

                for o in self.streams[e]:
                    for d in o.deps:
                        w(tsem[d.eng], d.sigval)
                    for ent, val in o.rawdeps:
                        if val > 0:
                            w(ent[0], val)
                    if o.fn is None:
                        continue
                    ins = o.fn(engobj)
                    if o.is_dma:
                        ins.then_inc(o.dsem[0], 16)
                    elif o.signaled:
                        ins.then_inc(tsem[e], 1)
                if e == SP:
                    for ent in self.dma_sems.values():
                        w(ent[0], 16 * ent[1])

            block.sync(lambda eng: run_stream(SP, eng))
            block.tensor(lambda eng: run_stream(PE, eng))
            block.scalar(lambda eng: run_stream(ACT, eng))
            block.vector(lambda eng: run_stream(DVE, eng))
            block.gpsimd(lambda eng: run_stream(POOL, eng))


class Arena:
    def __init__(self, nc, nbytes):
        self.nc = nc
        base = nc._sbuf_addr_for_side("left")
        base = (base + 63) // 64 * 64
        slab = nc.alloc_sbuf_tensor("arena", [128, nbytes // 4], F32)
        self.base = base
        self.top = base
        self.end = base + nbytes
        self.n = 0
        self.peak = base

    def alloc(self, name, shape, dt):
        sz = 2 if dt == BF16 else 4
        n = 1
        for s in shape[1:]:
            n *= s
        nbytes = (n * sz + 63) // 64 * 64
        off = self.top
        self.top += nbytes
        assert self.top <= self.end, "SBUF arena overflow at %s: %d > %d" % (name, self.top - self.base, self.end - self.base)
        self.peak = max(self.peak, self.top)
        self.n += 1
        self.last_off = off
        return self.nc.alloc_sbuf_tensor_at("%s_%d" % (name, self.n), shape, dt, offset=off)

    def alloc_abs(self, name, shape, dt, off):
        self.n += 1
        return self.nc.alloc_sbuf_tensor_at("%s_%d" % (name, self.n), shape, dt, offset=off)

    def alloc_top(self, name, shape, dt, off_from_end):
        self.n += 1
        return self.nc.alloc_sbuf_tensor_at("%s_%d" % (name, self.n), shape, dt, offset=self.end - off_from_end)

    def limit(self, reserve):
        assert self.top <= self.end - reserve, "arena reserve violated: %d > %d" % (self.top - self.base, self.end - reserve - self.base)

    def mark(self):
        return self.top

    def release(self, m):
        print("[arena] release at %.1f KiB -> %.1f KiB" % ((self.top - self.base) / 1024.0, (m - self.base) / 1024.0))
        self.top = m


def build_program(debug=False):
    nc = bass.Bass("TRN2", target_bir_lowering=False)

    def din(name, shape, dt=F32):
        return nc.dram_tensor(name, shape, dt, kind="ExternalInput").ap()

    x_d = din("x", [T, D])
    ctx_d = din("ctx", [CT, D])
    pos_d = din("pos", [T, D])
    vecs_d = din("vecs", [128, NV])
    bmod_d = din("bmod", [1, 6 * D])
    gfin_d = din("gfin", [1, D])
    wmod_d = din("wmod", [12, 128, 8, 512])
    win_d = din("win", [36, 128, 8, 128])
    wg_d = din("wg", [8, 128, 4, 128])
    wlru_d = din("wlru", [8, 128, 8, 128])
    wfo_d = din("wfo", [8, 128, 4, 128])
    wout_d = din("wout", [128, 8, D])
    w1_d = din("w1", [32, 128, 8, 128])
    w2_d = din("w2", [2, 8, 128, 4, 512])
    fc_d = din("fc", [128, 256], BF16)
    ct_d = din("ctab", [2, 128, 16, 512], BF16)
    st_d = din("stab", [2, 128, 16, 512], BF16)
    cn_d = din("cnyq", [128, 2], BF16)
    pos2_d = din("pos2", [T, 2, D], BF16)
    ident_d = din("ident", [128, 128], BF16)
    identf_d = din("identf", [128, 128], F32)
    out_d = nc.dram_tensor("out", [T, D], F32, kind="ExternalOutput").ap()
    xp_d = nc.dram_tensor("xp_scr", [T, D], F32, kind="Internal").ap()
    x1_d = nc.dram_tensor("x1_scr", [T, D], F32, kind="Internal").ap()
    gt_d = nc.dram_tensor("gt_scr", [1, 2048], F32, kind="Internal").ap()
    dbg = {}
    if debug:
        dbg["hT"] = nc.dram_tensor("dbg_hT", [128, 8, T], BF16, kind="ExternalOutput").ap()
        dbg["yfT"] = nc.dram_tensor("dbg_yfT", [128, 4, T], BF16, kind="ExternalOutput").ap()
        dbg["ylruT"] = nc.dram_tensor("dbg_ylruT", [128, 8, T], BF16, kind="ExternalOutput").ap()
        dbg["mergedT"] = nc.dram_tensor("dbg_mergedT", [128, 8, T], BF16, kind="ExternalOutput").ap()
        dbg["h2T"] = nc.dram_tensor("dbg_h2T", [128, 8, T], BF16, kind="ExternalOutput").ap()
        dbg["hv"] = nc.dram_tensor("dbg_hv", [128, 128], F32, kind="ExternalOutput").ap()
        dbg["siluc"] = nc.dram_tensor("dbg_siluc", [128, 16], BF16, kind="ExternalOutput").ap()
        dbg["modL"] = nc.dram_tensor("dbg_modL", [128, 32], F32, kind="ExternalOutput").ap()
        dbg["vecs"] = nc.dram_tensor("dbg_vecs", [128, NV], F32, kind="ExternalOutput").ap()
        dbg["gt"] = nc.dram_tensor("dbg_gt", [128, D], F32, kind="ExternalOutput").ap()

    P = Prog(nc)
    A = Arena(nc, 207 * 1024)
    banks = [nc.alloc_psum_tensor("bank%d" % i, [128, 512], F32) for i in range(8)]
    R_bank = [Res("bank%d" % i) for i in range(8)]
    bstate = {"i": 0}
    finals = []

    def nbank():
        i = bstate["i"]
        bstate["i"] = (i + 1) % 8
        return banks[i], R_bank[i]

    def dma(eng, out, in_, reads=(), writes=(), accum=None, sem=None):
        if accum is None:
            return P.op(eng, lambda e: e.dma_start(out=out, in_=in_), reads, writes, dma=True, dkey=sem)
        return P.op(eng, lambda e: e.dma_start(out=out, in_=in_, accum_op=accum), reads, writes, dma=True, dkey=sem)

    def mm(out, lhsT, rhs, start, stop, reads, writes):
        return P.op(PE, lambda e: e.matmul(out, lhsT=lhsT, rhs=rhs, start=start, stop=stop), reads, writes)

    def tr(out, in_, ident, reads, writes):
        return P.op(PE, lambda e: e.transpose(out, in_, ident), reads, writes)

    def act(out, in_, func, reads, writes, bias=None, scale=None, accum=None):
        kw = {}
        if bias is not None:
            kw["bias"] = bias
        if scale is not None:
            kw["scale"] = scale
        if accum is not None:
            kw["accum_out"] = accum
        return P.op(ACT, lambda e: e.activation(out=out, in_=in_, func=func, **kw), reads, writes)

    def ts(eng, out, in0, s1, s2, op0, op1, reads, writes):
        if s2 is None:
            return P.op(eng, lambda e: e.tensor_scalar(out=out, in0=in0, scalar1=s1, scalar2=None, op0=op0), reads, writes)
        return P.op(eng, lambda e: e.tensor_scalar(out=out, in0=in0, scalar1=s1, scalar2=s2, op0=op0, op1=op1), reads, writes)

    def tt(eng, out, in0, in1, op, reads, writes):
        return P.op(eng, lambda e: e.tensor_tensor(out=out, in0=in0, in1=in1, op=op), reads, writes)

    def stt(out, in0, scalar, in1, op0, op1, reads, writes):
        return P.op(DVE, lambda e: e.scalar_tensor_tensor(out=out, in0=in0, scalar=scalar, in1=in1, op0=op0, op1=op1),
                    reads, writes)

    def scan(out, d0, d1, init, reads, writes):
        return P.op(DVE, lambda e: e.tensor_tensor_scan(out=out, data0=d0, data1=d1, initial=init,
                                                        op0=ALU.mult, op1=ALU.add), reads, writes)

    def recip(out, in_, reads, writes):
        return P.op(DVE, lambda e: e.reciprocal(out=out, in_=in_), reads, writes)

    def memset(eng, ap, val, writes):
        return P.op(eng, lambda e: e.memset(ap, val), (), writes)

    def act_fence(writes):
        return P.op(ACT, lambda e: e.activation(out=fence_o[:], in_=fence_t[:], func=AF.Identity), [R_fence], writes)

    def copy(eng, out, in_, reads, writes):
        return P.op(eng, lambda e: e.tensor_copy(out=out, in_=in_), reads, writes)

    STG = {"tiles": [], "R": [], "i": 0, "ce": 0}

    def stg_setup(n):
        STG["tiles"] = [A.alloc("stg", [128, 1024], F32) for _ in range(n)]
        STG["R"] = [Res("stg%d" % i) for i in range(n)]
        STG["i"] = 0

    def load_w(dst, src, a, b, R_dst, engs):
        ga = max(1, 1024 // b)
        a0 = 0
        while a0 < a:
            g = min(ga, a - a0)
            sl = STG["i"] % len(STG["tiles"])
            STG["i"] += 1
            sv = STG["tiles"][sl][:, 0:g * b].rearrange("p (a b) -> p a b", a=g)
            dma(SP, sv, src[:, a0:a0 + g, :], writes=[STG["R"][sl]], sem="stg%d" % sl)
            eng = engs[STG["ce"] % len(engs)]
            STG["ce"] += 1
            if eng == ACT:
                act(dst[:, a0:a0 + g, :], sv, AF.Identity, [STG["R"][sl]], [R_dst])
            else:
                copy(eng, dst[:, a0:a0 + g, :], sv, [STG["R"][sl]], [R_dst])
            a0 += g

    vecs = A.alloc("vecs", [128, NV], F32)
    R_vecs = Res("vecs")
    hv = A.alloc("hv", [128, 128], F32)
    R_hv = Res("hv")
    ident = A.alloc("ident", [128, 128], BF16)
    R_ident = Res("ident")
    identf = A.alloc("identf", [128, 128], F32)
    R_identf = Res("identf")
    fence_t = A.alloc("fence_t", [128, 1], F32)
    fence_o = A.alloc("fence_o", [128, 1], F32)
    R_fence = Res("fence")
    siluc = A.alloc("siluc", [128, 16], BF16)
    R_siluc = Res("siluc")
    modL = A.alloc("modL", [128, 32], F32)
    modC = A.alloc("modC", [128, 16], F32)
    R_mod = Res("mod")
    R_gtd = Res("gtd")
    hcT = A.alloc("hcT", [128, 8, CT], BF16)
    R_hcT = Res("hcT")
    hT = A.alloc("hT", [128, 8, T], BF16)
    R_hT = [Res("hT%d" % g) for g in range(4)]
    MARK_E = A.mark()
    ylruT = A.alloc("ylruT", [128, 8, T], BF16)
    OFF_YLRU = A.last_off
    R_ylru = [Res("ylru%d" % h) for h in range(8)]
    yfT = A.alloc("yfT", [128, 4, T], BF16)
    R_yf = [Res("yf%d" % k) for k in range(4)]

    GMOD1, SH1, GMOD1C, SH1C, GMOD2, SH2 = 0, 8, 16, 24, 32, 40
    HBA, HBX, CS, HCS, NQ, QQ, EPSC = 48, 64, 80, 96, 112, 113, 114
    V_C, V_CC, V_BM, V_GMIX, V_GMLP, V_CW, V_CB, V_BA, V_BX, V_LAM = 0, 8, 16, 64, 72, 80, 112, 120, 136, 152

    dma(SP, vecs[:], vecs_d, writes=[R_vecs], sem="const")
    dma(SP, ident[:], ident_d, writes=[R_ident], sem="const")
    dma(SP, identf[:], identf_d, writes=[R_identf], sem="const")
    memset(DVE, fence_t[:], 0.0, [R_fence])
    memset(DVE, hv[:, NQ:NQ + 1], -0.25, [R_hv])
    memset(DVE, hv[:, QQ:QQ + 1], 0.25, [R_hv])
    memset(DVE, hv[:, EPSC:EPSC + 1], EPS, [R_hv])
    act(siluc[:], vecs[:, 0:16], AF.Silu, [R_vecs], [R_siluc])
    act(hv[:, CS:CS + 16], vecs[:, V_LAM:V_LAM + 16], AF.Exp, [R_vecs], [R_hv], scale=-1.0)
    act(hv[:, CS:CS + 16], hv[:, CS:CS + 16], AF.Ln, [R_hv], [R_hv], bias=1.0)
    ts(DVE, hv[:, HCS:HCS + 16], hv[:, CS:CS + 16], -4.0, None, ALU.mult, None, [R_hv], [R_hv])
    ts(DVE, hv[:, CS:CS + 16], hv[:, CS:CS + 16], -8.0, None, ALU.mult, None, [R_hv], [R_hv])
    ts(DVE, hv[:, HBA:HBA + 16], vecs[:, V_BA:V_BA + 16], 0.5, None, ALU.mult, None, [R_vecs], [R_hv])
    ts(DVE, hv[:, HBX:HBX + 16], vecs[:, V_BX:V_BX + 16], 0.5, None, ALU.mult, None, [R_vecs], [R_hv])

    WM = {}

    def mod_setup(n):
        WM["t"] = [A.alloc("wm", [128, 8, 512], BF16) for _ in range(n)]
        WM["R"] = [Res("wm%d" % i) for i in range(n)]
        WM["i"] = 0
        WM["gtrow"] = A.alloc("gtrow", [1, 512], F32)
        WM["R_gtrow"] = Res("gtrow")
        WM["mtmp"] = A.alloc("mtmp", [128, 8], F32)
        WM["R_mtmp"] = Res("mtmp")
        WM["bmrow"] = A.alloc("bmrow", [1, 512], F32)
        WM["R_bmrow"] = Res("bmrow")

    def mod_load(b, engs):
        s_ = WM["i"] % len(WM["t"])
        WM["i"] += 1
        WM["cur"] = s_
        if engs is None:
            dma(POOL, WM["t"][s_][:], wmod_d[b], writes=[WM["R"][s_]], sem="wmsw%d" % s_)
        else:
            load_w(WM["t"][s_][:], wmod_d[b], 8, 512, WM["R"][s_], engs)

    def mod_block(b, engs):
        mod_load(b, engs)
        mod_compute(b)

    def mod_compute(b):
        s_ = WM["cur"]
        wm, R_wm = WM["t"][s_], WM["R"][s_]
        pm, R_pm = nbank()
        if b in (0, 1, 2, 3, 6, 7, 8, 9):
            for j in range(4):
                for k in range(8):
                    mm(pm[:, 2 * j:2 * j + 2], wm[:, k, j * 128:(j + 1) * 128], siluc[:, k:16:8],
                       k == 0, k == 7, [R_wm, R_siluc], [R_pm])
            c0 = b * 4 if b < 4 else 16 + (b - 6) * 4
            vb = V_BM + b * 4
            act(WM["mtmp"][:, 0:8], pm[:, 0:8], AF.Identity, [R_pm], [WM["R_mtmp"]])
            tt(DVE, modL[:, c0:c0 + 4], WM["mtmp"][:, 0:8:2], vecs[:, vb:vb + 4], ALU.add, [WM["R_mtmp"], R_vecs], [R_mod])
            if b < 4:
                tt(DVE, modC[:, c0:c0 + 4], WM["mtmp"][:, 1:8:2], vecs[:, vb:vb + 4], ALU.add, [WM["R_mtmp"], R_vecs], [R_mod])
        else:
            gi = {4: 0, 5: 1, 10: 2, 11: 3}[b]
            gtrow, R_gtrow, bmrow, R_bmrow = WM["gtrow"], WM["R_gtrow"], WM["bmrow"], WM["R_bmrow"]
            dma(SP, bmrow[:], bmod_d[:, b * 512:(b + 1) * 512], writes=[R_bmrow], sem="bmrow")
            for k in range(8):
                mm(pm[0:1, :], siluc[:, k:k + 1], wm[:, k, :], k == 0, k == 7, [R_wm, R_siluc], [R_pm])
            act(gtrow[:], pm[0:1, :], AF.Identity, [R_pm], [R_gtrow])
            tt(DVE, gtrow[:], gtrow[:], bmrow[:], ALU.add, [R_gtrow, R_bmrow], [R_gtrow])
            dma(SP, gt_d[:, gi * 512:(gi + 1) * 512], gtrow[:], reads=[R_gtrow], writes=[R_gtd], sem="gtd")

    def mod_finish2():
        stt(hv[:, GMOD2:GMOD2 + 8], modL[:, 24:32], 1.0, vecs[:, V_GMLP:V_GMLP + 8], ALU.add, ALU.mult, [R_mod, R_vecs], [R_hv])
        copy(DVE, hv[:, SH2:SH2 + 8], modL[:, 16:24], [R_mod], [R_hv])

    mA = A.mark()
    mod_setup(2)
    stg_setup(3)
    for b in range(4):
        mod_block(b, [DVE, ACT])
    stt(hv[:, GMOD1:GMOD1 + 8], modL[:, 8:16], 1.0, vecs[:, V_GMIX:V_GMIX + 8], ALU.add, ALU.mult, [R_mod, R_vecs], [R_hv])
    copy(DVE, hv[:, SH1:SH1 + 8], modL[:, 0:8], [R_mod], [R_hv])
    stt(hv[:, GMOD1C:GMOD1C + 8], modC[:, 8:16], 1.0, vecs[:, V_GMIX:V_GMIX + 8], ALU.add, ALU.mult, [R_mod, R_vecs], [R_hv])
    copy(DVE, hv[:, SH1C:SH1C + 8], modC[:, 0:8], [R_mod], [R_hv])

    def nt_stage1(tiles, R_tiles, ss, R_ss, rstd, sq, xs, R_xs, n, dve_share=True):
        for i in range(n):
            act(xs[:, i, :], tiles[i], AF.Square, [R_tiles[i]], [R_xs, R_ss], accum=ss[:, i:i + 1])
        act_fence([R_ss])
        act(sq[:, 0:n], ss[:, 0:n], AF.Sqrt, [R_ss, R_hv], [R_ss], scale=1.0 / D, bias=hv[:, EPSC:EPSC + 1])
        recip(rstd[:, 0:n], sq[:, 0:n], [R_ss], [R_ss])
        for i in range(n):
            if dve_share and i % 2 == 0:
                ts(DVE, xs[:, i, :], tiles[i], rstd[:, i:i + 1], None, ALU.mult, None, [R_tiles[i], R_ss], [R_xs])
            else:
                act(xs[:, i, :], tiles[i], AF.Identity, [R_tiles[i], R_ss], [R_xs], scale=rstd[:, i:i + 1])

    def nt_stage2(xs, R_xs, dstT, dst_off, R_dst, gcol, scol, n, dve_share=True):
        for k in range(8):
            pb, R_pb = nbank()
            pT = pb[:].bitcast(BF16)
            for i in range(n):
                tr(pT[:, i * 128:(i + 1) * 128], xs[:, i, k * 128:(k + 1) * 128], ident[:], [R_xs, R_ident], [R_pb])
            dst = dstT[:, k, dst_off:dst_off + n * 128]
            if (k % 4 != 3) if dve_share else (k % 2 == 0):
                ts(DVE, dst, pT[:, 0:n * 128], hv[:, gcol + k:gcol + k + 1], hv[:, scol + k:scol + k + 1],
                   ALU.mult, ALU.add, [R_pb, R_hv], [R_dst])
            else:
                act(dst, pT[:, 0:n * 128], AF.Identity, [R_pb, R_hv], [R_dst],
                    scale=hv[:, gcol + k:gcol + k + 1], bias=hv[:, scol + k:scol + k + 1])

    NPT, NXT = 3, 6
    pt = [A.alloc("pt", [128, D], F32) for _ in range(NPT)]
    R_pt = [Res("pt%d" % i) for i in range(NPT)]
    xt = [A.alloc("xt", [128, D], F32) for _ in range(NXT)]
    R_xt = [Res("xt%d" % i) for i in range(NXT)]
    TOPC = 34 * 1024
    ctab0 = A.alloc_top("ctab0", [128, 16, 512], BF16, 34 * 1024)
    stab0 = A.alloc_top("stab0", [128, 16, 512], BF16, 18 * 1024)
    wf0 = A.alloc_top("wf0", [128, 8, 128], BF16, 2 * 1024)
    R_ctab = [Res("ctab%d" % i) for i in range(2)]
    R_stab = [Res("stab%d" % i) for i in range(2)]
    R_wf = [Res("wf%d" % i) for i in range(2)]
    xs = [A.alloc("xs", [128, 4, D], BF16) for _ in range(2)]
    R_xs = [Res("xs%d" % i) for i in range(2)]
    ssA = [A.alloc("ssA", [128, 4], F32) for _ in range(2)]
    sqA = [A.alloc("sqA", [128, 4], F32) for _ in range(2)]
    rsA = [A.alloc("rsA", [128, 4], F32) for _ in range(2)]
    R_ssA = [Res("ssA%d" % i) for i in range(2)]
    R_xpd = [Res("xpd%d" % i) for i in range(16)]

    def A_s1(gi):
        sl = gi % 2
        tl, Rl = [], []
        if gi == 0:
            for i in range(2):
                dma(SP, xt[i][:], ctx_d[i * 128:(i + 1) * 128, :], writes=[R_xt[i]])
                tl.append(xt[i][:])
                Rl.append(R_xt[i])
        else:
            g = gi - 1
            for i in range(4):
                ti = g * 4 + i
                s_ = (2 + ti) % NXT
                dma(SP, xt[s_][:], x_d[ti * 128:(ti + 1) * 128, :], writes=[R_xt[s_]])
                dma(SP, pt[ti % NPT][:], pos_d[ti * 128:(ti + 1) * 128, :], writes=[R_pt[ti % NPT]])
                tt(DVE, xt[s_][:], xt[s_][:], pt[ti % NPT][:], ALU.add, [R_xt[s_], R_pt[ti % NPT]], [R_xt[s_]])
                tl.append(xt[s_][:])
                Rl.append(R_xt[s_])
        nt_stage1(tl, Rl, ssA[sl], R_ssA[sl], rsA[sl], sqA[sl], xs[sl], R_xs[sl], len(tl), dve_share=False)

    def A_s2(gi):
        sl = gi % 2
        if gi == 0:
            nt_stage2(xs[sl], R_xs[sl], hcT, 0, R_hcT, GMOD1C, SH1C, 2, dve_share=False)
        else:
            g = gi - 1
            nt_stage2(xs[sl], R_xs[sl], hT, g * 512, R_hT[g], GMOD1, SH1, 4, dve_share=False)

    A_s1(0)
    for gi in range(5):
        if gi + 1 < 5:
            A_s1(gi + 1)
        A_s2(gi)
        if gi == 2:
            load_w(wf0[:], win_d[16], 8, 128, R_wf[0], [DVE])
            dma(SP, ctab0[:], ct_d[0], writes=[R_ctab[0]])
            dma(SP, stab0[:], st_d[0], writes=[R_stab[0]])
    A.limit(TOPC)
    P.barrier()
    A.release(mA)

    if debug:
        finals.append(dma(SP, dbg["hT"], hT[:], reads=R_hT, sem="dbg"))
        finals.append(dma(SP, dbg["hv"], hv[:], reads=[R_hv], sem="dbg"))
        finals.append(dma(SP, dbg["siluc"], siluc[:], reads=[R_siluc], sem="dbg"))
        finals.append(dma(SP, dbg["modL"], modL[:], reads=[R_mod], sem="dbg"))
        finals.append(dma(SP, dbg["vecs"], vecs[:], reads=[R_vecs], sem="dbg"))
    P.barrier()

    mC = A.mark()
    Z = A.alloc("Z", [128, 16, 4, 256], BF16)
    R_Z = [Res("Z%d" % g) for g in range(4)]
    stg_setup(2)
    fcm = A.alloc("fcm", [128, 256], BF16)
    R_fcm = Res("fcm")
    dma(SP, fcm[:], fc_d, writes=[R_fcm], sem="const")
    wf = [wf0, A.alloc("wf", [128, 8, 128], BF16)]
    ufT1 = A.alloc("ufT", [128, T], BF16)
    ufT = [ufT1, ufT1]
    R_uf1 = Res("uf")
    R_uf = [R_uf1, R_uf1]
    ctab = [ctab0, A.alloc("ctab", [128, 16, 512], BF16)]
    stab = [stab0, A.alloc("stab", [128, 16, 512], BF16)]
    cnq = A.alloc("cnq", [128, 2], BF16)
    R_cnq = Res("cnq")
    dma(SP, cnq[:], cn_d, writes=[R_cnq], sem="const")
    psb = [A.alloc("psb", [128, 512], F32) for _ in range(2)]
    R_psb = [Res("psb%d" % i) for i in range(2)]
    dma(SP, ctab[1][:], ct_d[1], writes=[R_ctab[1]])
    dma(SP, stab[1][:], st_d[1], writes=[R_stab[1]])
    A.limit(TOPC)
    for g in range(4):
        s = g % 2
        if g + 1 < 4:
            load_w(wf[1 - s][:], win_d[16 + g + 1], 8, 128, R_wf[1 - s], [DVE])
        for blk in range(4):
            pb, R_pb = nbank()
            for k in range(8):
                mm(pb[:], wf[s][:, k, :], hT[:, k, blk * 512:(blk + 1) * 512], k == 0, k == 7,
                   [R_wf[s], R_hT[blk]], [R_pb])
            if blk % 2 == 0:
                act(ufT[s][:, blk * 512:(blk + 1) * 512], pb[:], AF.Identity, [R_pb], [R_uf[s]])
            else:
                copy(DVE, ufT[s][:, blk * 512:(blk + 1) * 512], pb[:], [R_pb], [R_uf[s]])
        for t2 in range(8):
            pb, R_pb = nbank()
            for i in range(2):
                ti = t2 * 2 + i
                mm(pb[:, i * 256:(i + 1) * 256], ufT[s][:, ti * 128:(ti + 1) * 128], fcm[:], True, True,
                   [R_uf[s], R_fcm], [R_pb])
            zin = pb[:].rearrange("p (a b) -> p a b", a=2)
            if t2 % 2 == 0:
                act(Z[:, t2 * 2:t2 * 2 + 2, g, :], zin, AF.Identity, [R_pb], [R_Z[g]])
            else:
                copy(DVE, Z[:, t2 * 2:t2 * 2 + 2, g, :], zin, [R_pb], [R_Z[g]])
    pnq, R_pnq = nbank()
    for g in range(4):
        for j in range(16):
            mm(pnq[:, 2 * g:2 * g + 2], Z[:, j, g, 0:128], cnq[:], j == 0, j == 15, [R_Z[g], R_cnq], [R_pnq])
    act(yfT[:, :, 1024:1025], pnq[:, 0:8:2].rearrange("p (g o) -> p g o", o=1), AF.Identity, [R_pnq], R_yf)
    YTOT = 4 * T
    for kt in range(2):
        for g in range(4):
            pP, R_pP = nbank()
            for j in range(16):
                mm(pP[:], Z[:, j, g, 0:128], ctab[kt][:, j, :], j == 0, j == 15, [R_Z[g], R_ctab[kt]], [R_pP])
            pQ, R_pQ = nbank()
            for j in range(16):
                mm(pQ[:], Z[:, j, g, 128:256], stab[kt][:, j, :], j == 0, j == 15, [R_Z[g], R_stab[kt]], [R_pQ])
            u = (kt * 4 + g) % 2
            act(psb[u][:], pP[:], AF.Identity, [R_pP], [R_psb[u]])
            tt(DVE, yfT[:, g, kt * 512:(kt + 1) * 512], pQ[:], psb[u][:], ALU.add, [R_pQ, R_psb[u]], [R_yf[g]])
            if kt == 0:
                outm = bass.AP(yfT, g * T + (T - 1), [[YTOT, 128], [-1, 511]])
                tt(DVE, outm, psb[u][:, 1:512], pQ[:, 1:512], ALU.subtract, [R_pQ, R_psb[u]], [R_yf[g]])
            else:
                outm = bass.AP(yfT, g * T + 1536, [[YTOT, 128], [-1, 512]])
                tt(DVE, outm, psb[u][:], pQ[:], ALU.subtract, [R_pQ, R_psb[u]], [R_yf[g]])
    if debug:
        finals.append(dma(SP, dbg["yfT"], yfT[:], reads=R_yf, sem="dbg"))
    P.barrier()
    A.limit(TOPC)
    A.release(mC)

    mB = A.mark()
    TL = CT + T
    XL = TL + CT
    wx = [A.alloc("wx", [128, 8, 128], BF16) for _ in range(2)]
    wy = [A.alloc("wy", [128, 8, 128], BF16) for _ in range(2)]
    wgt = [A.alloc("wgt", [128, 4, 128], BF16) for _ in range(2)]
    dg = [A.alloc("dg", [128, 4, 128], BF16) for _ in range(2)]
    R_wx = [Res("wx%d" % i) for i in range(2)]
    R_wy = [Res("wy%d" % i) for i in range(2)]
    R_wgt = [Res("wgt%d" % i) for i in range(2)]
    R_dg = [Res("dg%d" % i) for i in range(2)]
    uxp = A.alloc("uxp", [128, T + 4], BF16)
    uxcp = A.alloc("uxcp", [128, CT + 4], BF16)
    R_uxp = [Res("uxp%d" % i) for i in range(5)]
    xc = [A.alloc("xc", [128, XL], BF16) for _ in range(2)]
    R_xc = [[Res("xc%d_%d" % (i, b)) for b in range(5)] for i in range(2)]
    thi = [A.alloc("thi", [128, TL], BF16) for _ in range(2)]
    aa = [A.alloc("aa", [128, TL], F32) for _ in range(2)]
    thr = [A.alloc("thr", [128, TL], F32) for _ in range(2)]
    mult = [A.alloc("mult", [128, TL], BF16) for _ in range(2)]
    R_thi = [[Res("thi%d_%d" % (i, b)) for b in range(5)] for i in range(2)]
    R_thr = [[Res("thr%d_%d" % (d, b)) for b in range(5)] for d in range(2)]
    R_aa = [Res("aa%d" % i) for i in range(2)]
    R_mult = [Res("mult%d" % i) for i in range(2)]
    t1b = A.alloc("t1b", [128, TL], BF16)
    R_t1 = Res("t1b")
    hfb = [A.alloc("hfb", [128, TL], BF16) for _ in range(2)]
    R_hfb = [Res("hfb%d" % d) for d in range(2)]
    gl = A.alloc("gl", [128, T], BF16)
    R_gl = Res("gl")

    stg_setup(3)
    MODB = [6, 7, 8, 9, 4, 5, 10, 11]
    memset(DVE, uxp[:, 0:2], 0.0, [R_uxp[1]])
    memset(DVE, uxp[:, T + 2:T + 4], 0.0, [R_uxp[4]])
    memset(DVE, uxcp[:, 0:2], 0.0, [R_uxp[0]])
    memset(DVE, uxcp[:, CT + 2:CT + 4], 0.0, [R_uxp[0]])

    XOFF = [0] + [CT + b * 512 for b in range(4)]
    BLEN = [CT, 512, 512, 512, 512]
    DOFF = [[0] + [CT + b * 512 for b in range(4)], [T] + [b * 512 for b in range(4)]]

    def load_head_weights(h):
        s = h % 2
        load_w(wx[s][:], win_d[h], 8, 128, R_wx[s], [DVE])
        load_w(wy[s][:], win_d[8 + h], 8, 128, R_wy[s], [DVE])
        load_w(wgt[s][:], wg_d[h], 4, 128, R_wgt[s], [DVE])

    def front(h):
        s = h % 2
        for k in range(4):
            ts(DVE, dg[s][:, k, :], identf[:], vecs[:, V_CW + k * 8 + h:V_CW + k * 8 + h + 1], None, ALU.mult, None,
               [R_identf, R_vecs], [R_dg[s]])
        for b in range(5):
            n = BLEN[b]
            pb, R_pb = nbank()
            for k in range(8):
                rhs = hcT[:, k, :] if b == 0 else hT[:, k, (b - 1) * 512:b * 512]
                mm(pb[:, 0:n], wx[s][:, k, :], rhs, k == 0, k == 7,
                   [R_wx[s], R_hcT if b == 0 else R_hT[b - 1]], [R_pb])
            dst = uxcp[:, 2:2 + CT] if b == 0 else uxp[:, 2 + (b - 1) * 512:2 + b * 512]
            copy(DVE, dst, pb[:, 0:n], [R_pb], [R_uxp[b]])
        for b in range(5):
            n = BLEN[b]
            pb, R_pb = nbank()
            for k in range(4):
                if b == 0:
                    rhs = uxcp[:, k:k + CT]
                    rr = [R_uxp[0]]
                else:
                    st0 = (b - 1) * 512 + k
                    rhs = uxp[:, st0:st0 + 512]
                    rr = [R_uxp[j] for j in range(max(1, b - 1), min(4, b + 1) + 1)]
                mm(pb[:, 0:n], dg[s][:, k, :], rhs, k == 0, k == 3, [R_dg[s]] + rr, [R_pb])
            ts(DVE, xc[s][:, XOFF[b]:XOFF[b] + n], pb[:, 0:n], vecs[:, V_CB + h:V_CB + h + 1], None, ALU.add, None,
               [R_pb, R_vecs], [R_xc[s][b]])
            if b == 0:
                ts(DVE, xc[s][:, TL:TL + n], pb[:, 0:n], vecs[:, V_CB + h:V_CB + h + 1], None, ALU.add, None,
                   [R_pb, R_vecs], [R_xc[s][b]])

    def gates_te(h, d):
        s = h % 2
        q = d
        col = d * 8 + h
        for b in range(5):
            n = BLEN[b]
            xo, do = XOFF[b], DOFF[d][b]
            pr, R_pr = nbank()
            mm(pr[:, 0:n], wgt[s][:, d, :], xc[s][:, xo:xo + n], True, True, [R_wgt[s], R_xc[s][b]], [R_pr])
            pi, R_pi = nbank()
            mm(pi[:, 0:n], wgt[s][:, 2 + d, :], xc[s][:, xo:xo + n], True, True, [R_wgt[s], R_xc[s][b]], [R_pi])
            act(thr[d][:, do:do + n], pr[:, 0:n], AF.Tanh, [R_pr, R_hv], [R_thr[d][b]], scale=0.5,
                bias=hv[:, HBA + col:HBA + col + 1])
            act(thi[q][:, do:do + n], pi[:, 0:n], AF.Tanh, [R_pi, R_hv], [R_thi[q][b]], scale=0.5,
                bias=hv[:, HBX + col:HBX + col + 1])
        act(aa[q][:], thr[d][:], AF.Exp, R_thr[d] + [R_hv], [R_aa[q]],
            scale=hv[:, HCS + col:HCS + col + 1], bias=hv[:, HCS + col:HCS + col + 1])
        act(thr[d][:], thr[d][:], AF.Exp, R_thr[d] + [R_hv], R_thr[d],
            scale=hv[:, CS + col:CS + col + 1], bias=hv[:, CS + col:CS + col + 1])

    def gates_sqrt(h, d):
        q = d
        act(mult[q][:], thr[d][:], AF.Sqrt, R_thr[d] + [R_hv], [R_mult[q]],
            scale=hv[:, NQ:NQ + 1], bias=hv[:, QQ:QQ + 1])

    def dve_scan(h, d):
        s = h % 2
        q = d
        xview = xc[s][:, 0:TL] if d == 0 else xc[s][:, CT:XL]
        stt(t1b[:], thi[q][:], 1.0, xview, ALU.add, ALU.mult, R_thi[q] + R_xc[s], [R_t1])
        tt(DVE, t1b[:], t1b[:], mult[q][:], ALU.mult, [R_t1, R_mult[q]], [R_t1])
        if d == 0:
            scan(hfb[d][:], aa[q][:], t1b[:], 0.0, [R_aa[q], R_t1], [R_hfb[d]])
        else:
            scan(rev(hfb[d], 0, TL, TL), rev(aa[q], 0, TL, TL), rev(t1b, 0, TL, TL), 0.0, [R_aa[q], R_t1], [R_hfb[d]])

    def rev(tn, o0, nn, tot):
        return bass.AP(tn, o0 + nn - 1, [[tot, 128], [-1, nn]])

    def uy_gelu(h):
        s = h % 2
        for b in range(4):
            pb, R_pb = nbank()
            for k in range(8):
                mm(pb[:], wy[s][:, k, :], hT[:, k, b * 512:(b + 1) * 512], k == 0, k == 7, [R_wy[s], R_hT[b]], [R_pb])
            act(gl[:, b * 512:(b + 1) * 512], pb[:], AF.Gelu_apprx_tanh, [R_pb], [R_gl])

    def finish(h):
        tt(DVE, hfb[0][:, CT:TL], hfb[0][:, CT:TL], hfb[1][:, 0:T], ALU.add, R_hfb, [R_hfb[0]])
        tt(DVE, ylruT[:, h, :], hfb[0][:, CT:TL], gl[:], ALU.mult, [R_hfb[0], R_gl], [R_ylru[h]])

    TOPD = 6 * 1024
    wug0 = A.alloc_top("wug0", [128, 2, 8, 128], BF16, 6 * 1024)
    wlr0 = A.alloc_top("wlr0", [128, 8, 128], BF16, 2 * 1024)
    R_wug = [Res("wug%d" % i) for i in range(2)]
    R_wlr = [Res("wlr%d" % i) for i in range(2)]
    A.limit(TOPD)
    load_head_weights(0)
    front(0)
    for h in range(8):
        if h + 1 < 8:
            load_head_weights(h + 1)
        else:
            load_w(wug0[:, 0, :, :], win_d[20], 8, 128, R_wug[0], [DVE])
            load_w(wug0[:, 1, :, :], win_d[28], 8, 128, R_wug[0], [DVE])
            load_w(wlr0[:], wlru_d[0], 8, 128, R_wlr[0], [DVE])
        gates_te(h, 0)
        gates_te(h, 1)
        if h + 1 < 8:
            front(h + 1)
        gates_sqrt(h, 0)
        gates_sqrt(h, 1)
        uy_gelu(h)
        dve_scan(h, 0)
        dve_scan(h, 1)
        finish(h)
    if debug:
        finals.append(dma(SP, dbg["ylruT"], ylruT[:], reads=R_ylru, sem="dbg"))
    P.barrier()
    A.release(mB)

    mD = A.mark()
    mergedT = A.alloc("mergedT", [128, 8, T], BF16)
    R_mg = [Res("mg%d" % b) for b in range(4)]
    woutb = A.alloc("woutb", [128, 8, D], BF16)
    R_wout = Res("wout")
    stg_setup(3)
    wug = [wug0, A.alloc("wug", [128, 2, 8, 128], BF16)]
    OFF_WUG = A.last_off
    wlr = [wlr0, A.alloc("wlr", [128, 8, 128], BF16)]
    wfb = [A.alloc("wfb", [128, 4, 128], BF16) for _ in range(2)]
    R_wfb = [Res("wfb%d" % i) for i in range(2)]
    g1b = [A.alloc("g1b", [128, 512], BF16) for _ in range(2)]
    g2b = [A.alloc("g2b", [128, 512], BF16) for _ in range(2)]
    t1m = [A.alloc("t1m", [128, 512], BF16) for _ in range(2)]
    t2m = [A.alloc("t2m", [128, 512], BF16) for _ in range(2)]
    R_g1 = [Res("g1b%d" % i) for i in range(2)]
    R_g2 = [Res("g2b%d" % i) for i in range(2)]
    R_t1m = [Res("t1m%d" % i) for i in range(2)]
    R_t2m = [Res("t2m%d" % i) for i in range(2)]
    gt1bc = A.alloc("gt1bc", [128, D], F32)
    R_gt1bc = Res("gt1bc")
    xpt = [A.alloc("xpt", [128, D], F32) for _ in range(4)]
    R_xpt = [Res("xpt%d" % i) for i in range(4)]
    tmpd = [A.alloc("tmpd", [128, 512], F32) for _ in range(2)]
    R_tmpd = [Res("tmpd%d" % i) for i in range(2)]
    R_x1d = [Res("x1d%d" % i) for i in range(16)]

    mod_setup(1)

    def load_dc_weights(dc):
        s = dc % 2
        if dc > 0:
            load_w(wug[s][:, 0, :, :], win_d[20 + dc], 8, 128, R_wug[s], [ACT, DVE])
            load_w(wug[s][:, 1, :, :], win_d[28 + dc], 8, 128, R_wug[s], [ACT, DVE])
            load_w(wlr[s][:], wlru_d[dc], 8, 128, R_wlr[s], [ACT, DVE])
        dma(POOL, wfb[s][:], wfo_d[dc], writes=[R_wfb[s]], sem="wfsw%d" % s)

    WO_PENDING = []

    def wout_dma(k):
        sl = STG["i"] % len(STG["tiles"])
        STG["i"] += 1
        sv = STG["tiles"][sl][:, 0:D]
        dma(SP, sv, wout_d[:, k, :], writes=[STG["R"][sl]], sem="stg%d" % sl)
        WO_PENDING.append((k, sv, STG["R"][sl]))

    def wout_casts():
        while WO_PENDING:
            k, sv, Rs = WO_PENDING.pop(0)
            tt(DVE, woutb[:, k, :], sv, gt1bc[:], ALU.mult, [Rs, R_gt1bc], [R_wout])

    MODD = [4, 5, 6, 7, 8, 9, 10, 11]
    load_dc_weights(0)
    ci = 0
    for dc in range(8):
        s = dc % 2
        if dc + 1 < 8:
            load_dc_weights(dc + 1)
        mod_load(MODD[dc], None)
        if dc == 2:
            dma(SP, gt1bc[:], gt_d[:, 0:1024].broadcast_to([128, D]), reads=[R_gtd], writes=[R_gt1bc], sem="const2")
        if 3 <= dc <= 6:
            wout_dma(2 * (dc - 3))
            wout_dma(2 * (dc - 3) + 1)
        for b in range(4):
            u = ci % 2
            ci += 1
            tsl = slice(b * 512, (b + 1) * 512)
            p1, R_p1 = nbank()
            for k in range(8):
                mm(p1[:], wug[s][:, 0, k, :], hT[:, k, tsl], k == 0, k == 7, [R_wug[s], R_hT[b]], [R_p1])
            p2, R_p2 = nbank()
            for k in range(8):
                mm(p2[:], wug[s][:, 1, k, :], hT[:, k, tsl], k == 0, k == 7, [R_wug[s], R_hT[b]], [R_p2])
            p3, R_p3 = nbank()
            for k in range(8):
                mm(p3[:], wlr[s][:, k, :], ylruT[:, k, tsl], k == 0, k == 7, [R_wlr[s], R_ylru[k]], [R_p3])
            p4, R_p4 = nbank()
            for k in range(4):
                mm(p4[:], wfb[s][:, k, :], yfT[:, k, tsl], k == 0, k == 3, [R_wfb[s], R_yf[k]], [R_p4])
            act(g1b[u][:], p1[:], AF.Sigmoid, [R_p1], [R_g1[u]])
            act(g2b[u][:], p2[:], AF.Sigmoid, [R_p2], [R_g2[u]])
            tt(DVE, t1m[u][:], p3[:], g1b[u][:], ALU.mult, [R_p3, R_g1[u]], [R_t1m[u]])
            tt(DVE, t2m[u][:], p4[:], g2b[u][:], ALU.mult, [R_p4, R_g2[u]], [R_t2m[u]])
            tt(DVE, mergedT[:, dc, tsl], t1m[u][:], t2m[u][:], ALU.add, [R_t1m[u], R_t2m[u]], [R_mg[b]])
            if b == 1:
                wout_casts()
        mod_compute(MODD[dc])
        if dc == 5:
            mod_finish2()
    if debug:
        finals.append(dma(SP, dbg["mergedT"], mergedT[:], reads=R_mg, sem="dbg"))
    P.barrier()
    ptd = [A.alloc_abs("ptd", [128, 2, D], BF16, OFF_WUG + i * 4096) for i in range(4)]
    R_ptd = [Res("ptd%d" % i) for i in range(4)]
    xs2all = A.alloc_abs("xs2all", [128, 16, D], BF16, OFF_YLRU)
    R_xsa = [Res("xsa%d" % i) for i in range(4)]
    ssd = A.alloc("ssd", [128, 16], F32)
    sqd = A.alloc("sqd", [128, 16], F32)
    rsd = A.alloc("rsd", [128, 16], F32)
    R_ssd = [Res("ssd%d" % i) for i in range(4)]

    def D_s0(ti):
        s8 = ti % 4
        dma(SP, xpt[s8][:], x_d[ti * 128:(ti + 1) * 128, :], writes=[R_xpt[s8]])
        dma(SP, ptd[s8][:], pos2_d[ti * 128:(ti + 1) * 128, :, :], writes=[R_ptd[s8]])

    def D_s1(ti):
        s8 = ti % 4
        g4 = ti // 4
        for hh in range(2):
            pb, R_pb = nbank()
            for k in range(8):
                mm(pb[:], mergedT[:, k, ti * 128:(ti + 1) * 128], woutb[:, k, hh * 512:(hh + 1) * 512],
                   k == 0, False, [R_mg[ti // 4], R_wout], [R_pb])
            mm(pb[:], ident[:], ptd[s8][:, 0, hh * 512:(hh + 1) * 512], False, False, [R_ident, R_ptd[s8]], [R_pb])
            mm(pb[:], ident[:], ptd[s8][:, 1, hh * 512:(hh + 1) * 512], False, True, [R_ident, R_ptd[s8]], [R_pb])
            tt(DVE, xpt[s8][:, hh * 512:(hh + 1) * 512], pb[:], xpt[s8][:, hh * 512:(hh + 1) * 512], ALU.add,
               [R_pb, R_xpt[s8]], [R_xpt[s8]])
        dma(POOL, x1_d[ti * 128:(ti + 1) * 128, :], xpt[s8][:], reads=[R_xpt[s8]], writes=[R_x1d[ti]], sem="x1d%d" % s8)
        act(xs2all[:, ti, :], xpt[s8][:], AF.Square, [R_xpt[s8]], [R_xsa[g4], R_ssd[g4]], accum=ssd[:, ti:ti + 1])
        act_fence([R_ssd[g4]])
        act(sqd[:, ti:ti + 1], ssd[:, ti:ti + 1], AF.Sqrt, [R_ssd[g4], R_hv], [R_ssd[g4]], scale=1.0 / D, bias=hv[:, EPSC:EPSC + 1])
        recip(rsd[:, ti:ti + 1], sqd[:, ti:ti + 1], [R_ssd[g4]], [R_ssd[g4]])
        if ti % 2 == 0:
            ts(DVE, xs2all[:, ti, :], xpt[s8][:], rsd[:, ti:ti + 1], None, ALU.mult, None, [R_xpt[s8], R_ssd[g4]], [R_xsa[g4]])
        else:
            act(xs2all[:, ti, :], xpt[s8][:], AF.Identity, [R_xpt[s8], R_ssd[g4]], [R_xsa[g4]], scale=rsd[:, ti:ti + 1])

    D_s0(0)
    D_s0(1)
    for ti in range(16):
        if ti + 2 < 16:
            D_s0(ti + 2)
        D_s1(ti)
    w1t_top = [A.alloc_top("w1t", [128, 8, 128], BF16, (3 - i) * 2048) for i in range(3)]
    R_w1t = [Res("w1t%d" % i) for i in range(4)]
    for i in range(2):
        load_w(w1t_top[i][:], w1_d[i], 8, 128, R_w1t[i], [DVE, ACT])
    for g4 in range(4):
        nt_stage2(xs2all[:, g4 * 4:(g4 + 1) * 4, :], R_xsa[g4], hT, g4 * 512, R_hT[g4], GMOD2, SH2, 4)
    A.limit(TOPD)
    if debug:
        finals.append(dma(SP, dbg["h2T"], hT[:], reads=R_hT, sem="dbg"))
    P.barrier()
    A.release(mD)

    A.release(MARK_E)
    stg_setup(4)
    hid = A.alloc("hid", [128, 32, 1024], BF16)
    R_hid = [Res("hid%d" % i) for i in range(8)]
    x1y = A.alloc("x1y", [128, 8, D], F32)
    R_x1y = [Res("x1y%d" % j) for j in range(8)]
    w1t = w1t_top + [A.alloc("w1t", [128, 8, 128], BF16)]
    w2t = [A.alloc("w2t", [128, 4, 512], BF16) for _ in range(3)]
    R_w2t = [Res("w2t%d" % i) for i in range(3)]
    gt2bc = A.alloc("gt2bc", [128, D], F32)
    R_gt2bc = Res("gt2bc")
    gfbc = A.alloc("gfbc", [128, D], F32)
    R_gfbc = Res("gfbc")
    rl = [A.alloc("rl", [128, 512], BF16) for _ in range(3)]
    R_rl = [Res("rl%d" % i) for i in range(3)]
    tmpe = [A.alloc("tmpe", [128, 512], F32) for _ in range(2)]
    R_tmpe = [Res("tmpe%d" % i) for i in range(2)]
    junk3 = A.alloc("junk3", [128, 512], BF16)
    R_junk3 = Res("junk3")
    ssf = A.alloc("ssf", [128, 16], F32)
    R_ssf = Res("ssf")
    sqf = A.alloc("sqf", [128, 8], F32)
    rsf = A.alloc("rsf", [128, 8], F32)

    dma(SP, gt2bc[:], gt_d[:, 1024:2048].broadcast_to([128, D]), reads=[R_gtd], writes=[R_gt2bc], sem="const2")
    dma(SP, gfbc[:], gfin_d.broadcast_to([128, D]), writes=[R_gfbc], sem="const2")

    w1i = 0
    w2i = 0
    rli = 0
    tei = 0
    for tb in range(2):
        t0 = tb * 1024
        def ld1(fcn):
            load_w(w1t[fcn % 4][:], w1_d[fcn], 8, 128, R_w1t[fcn % 4], [DVE, ACT])

        def ld2(i):
            load_w(w2t[i % 3][:], w2_d[i // 8, i % 8], 4, 512, R_w2t[i % 3], [DVE, ACT])
        for fcn in range(32):
            s = fcn % 4
            if fcn + 2 < 32:
                ld1(fcn + 2)
            if fcn == 28:
                ld2(0)
                ld2(1)
            for nb in range(2):
                pb, R_pb = nbank()
                for k in range(8):
                    mm(pb[:], w1t[s][:, k, :], hT[:, k, t0 + nb * 512:t0 + (nb + 1) * 512], k == 0, k == 7,
                       [R_w1t[s], R_hT[tb * 2 + nb]], [R_pb])
                r = rli % 3
                rli += 1
                act(rl[r][:], pb[:], AF.Relu, [R_pb], [R_rl[r]])
                tt(DVE, hid[:, fcn, nb * 512:(nb + 1) * 512], rl[r][:], rl[r][:], ALU.mult,
                   [R_rl[r]], [R_hid[fcn // 4]])
        for j in range(8):
            ti = tb * 8 + j
            dma(SP, x1y[:, j, :], x1_d[ti * 128:(ti + 1) * 128, :], reads=[R_x1d[ti]], writes=[R_x1y[j]])
        for dh in range(2):
            dsl = slice(dh * 512, (dh + 1) * 512)
            for f4 in range(8):
                i2 = dh * 8 + f4
                s = i2 % 3
                if i2 + 2 < 16:
                    ld2(i2 + 2)
                for j in range(8):
                    for c in range(4):
                        fcn = f4 * 4 + c
                        mm(banks[j][:], hid[:, fcn, j * 128:(j + 1) * 128], w2t[s][:, c, :], fcn == 0, fcn == 31,
                           [R_hid[f4], R_w2t[s]], [R_bank[j]])
                if dh == 1 and f4 == 6 and tb == 0:
                    ld1(0)
                    ld1(1)
            for j in range(8):
                u = tei % 2
                tei += 1
                tt(DVE, tmpe[u][:], banks[j][:], gt2bc[:, dsl], ALU.mult, [R_bank[j], R_gt2bc], [R_tmpe[u]])
                tt(DVE, x1y[:, j, dsl], tmpe[u][:], x1y[:, j, dsl], ALU.add,
                   [R_tmpe[u], R_x1y[j]], [R_x1y[j]])
                act(junk3[:], x1y[:, j, dsl], AF.Square, [R_x1y[j]], [R_junk3, R_ssf], accum=ssf[:, dh * 8 + j:dh * 8 + j + 1])
            act_fence([R_ssf])
        tt(DVE, ssf[:, 0:8], ssf[:, 0:8], ssf[:, 8:16], ALU.add, [R_ssf], [R_ssf])
        act(sqf[:], ssf[:, 0:8], AF.Sqrt, [R_ssf, R_hv], [R_ssf], scale=1.0 / D, bias=hv[:, EPSC:EPSC + 1])
        recip(rsf[:], sqf[:], [R_ssf], [R_ssf])
        for j in range(8):
            ti = tb * 8 + j
            for hh in range(2):
                hs = slice(hh * 512, (hh + 1) * 512)
                stt(x1y[:, j, hs], x1y[:, j, hs], rsf[:, j:j + 1], gfbc[:, hs], ALU.mult, ALU.mult,
                    [R_x1y[j], R_ssf, R_gfbc], [R_x1y[j]])
            finals.append(dma(POOL, out_d[ti * 128:(ti + 1) * 128, :], x1y[:, j, :], reads=[R_x1y[j]], sem="outst%d" % j))

    P.emit(finals)
    print("[build] ops:", {e: len(v) for e, v in P.streams.items()}, "dma sems:", len(P.dma_sems),
          "sbuf peak KiB:", (A.peak - A.base) / 1024.0)
    return nc


def _pos_table():
    def sincos(n, dim):
        half = dim // 2
        freqs = np.exp(-math.log(10000.0) * np.arange(half, dtype=np.float32) / half).astype(np.float32)
        ang = np.arange(n, dtype=np.float32)[:, None] * freqs[None, :]
        return np.concatenate([np.sin(ang), np.cos(ang)], axis=-1).astype(np.float32)
    rows = T // 64
    er = sincos(rows, D // 2)
    ec = sincos(64, D // 2)
    emb = np.concatenate([np.broadcast_to(er[:, None, :], (rows, 64, D // 2)),
                          np.broadcast_to(ec[None, :, :], (rows, 64, D // 2))], axis=-1)
    return np.ascontiguousarray(emb.reshape(T, D).astype(np.float32))


def _dft_tables():
    bf = ml_dtypes.bfloat16
    n = np.arange(128)
    angc = 2 * np.pi * np.outer(n, n) / 128.0
    fc = np.concatenate([np.cos(angc), -np.sin(angc)], axis=1) / math.sqrt(128.0)
    t = np.arange(T)
    angt = 2 * np.pi * (np.outer(t, t) % T) / T
    ctab = (np.cos(angt) / math.sqrt(T)).astype(np.float32)
    stab = (np.sin(angt) / math.sqrt(T)).astype(np.float32)

    def lay(m):
        return np.ascontiguousarray(m.reshape(16, 128, 4, 512).transpose(2, 1, 0, 3)[0:2]).astype(bf)
    cn = np.zeros((128, 2), np.float32)
    cn[:, 0] = ((-1.0) ** np.arange(128)) / math.sqrt(T)
    return fc.astype(np.float32).astype(bf), lay(ctab), lay(stab), cn.astype(bf)


def _kp(w, nblk, bw):
    K, N = w.shape
    return np.ascontiguousarray(w.reshape(K // 128, 128, nblk, bw).transpose(2, 1, 0, 3))


def _fm(v):
    return np.ascontiguousarray(np.asarray(v, np.float32).reshape(-1, 128).T)


_CACHE = {}


def kernel(x, c, ctx, c_ctx, w_mod, b_mod, g_mix, w_in, conv_w, conv_b, w_a, b_a, w_x, b_x,
           lam, w_lru_out, w_f_out, w_out, g_mlp, w1, w2, g_final):
    f = lambda a: np.asarray(a, np.float32)
    x, c, ctx, c_ctx = f(x), f(c), f(ctx), f(c_ctx)
    w_mod, b_mod, g_mix, w_in = f(w_mod)[0], f(b_mod)[0], f(g_mix)[0], f(w_in)[0]
    conv_w, conv_b, w_a, b_a, w_x, b_x, lam = f(conv_w)[0], f(conv_b)[0], f(w_a)[0], f(b_a)[0], f(w_x)[0], f(b_x)[0], f(lam)[0]
    w_lru_out, w_f_out, w_out, g_mlp, w1, w2, g_final = f(w_lru_out)[0], f(w_f_out)[0], f(w_out)[0], f(g_mlp)[0], f(w1)[0], f(w2)[0], f(g_final)
    B = x.shape[0]

    if "consts" not in _CACHE:
        fc, ctab, stab, cnyq = _dft_tables()
        pos = _pos_table()
        bf = ml_dtypes.bfloat16
        pos_hi = pos.astype(bf)
        pos_lo = (pos - pos_hi.astype(np.float32)).astype(bf)
        pos2 = np.ascontiguousarray(np.stack([pos_hi, pos_lo], axis=1))
        _CACHE["consts"] = dict(pos=pos, pos2=pos2, fc=fc, ctab=ctab, stab=stab, cnyq=cnyq,
                                ident=np.eye(128, dtype=np.float32).astype(ml_dtypes.bfloat16),
                                identf=np.eye(128, dtype=np.float32))
    cst = _CACHE["consts"]

    shared = dict(cst)
    shared["bmod"] = np.ascontiguousarray(b_mod.reshape(1, -1))
    shared["gfin"] = np.ascontiguousarray(g_final.reshape(1, -1))
    shared["wmod"] = _kp(w_mod, 12, 512)
    shared["win"] = _kp(w_in, 36, 128)
    wg = np.stack([w_a[0], w_a[1], w_x[0], w_x[1]], axis=0)
    shared["wg"] = np.ascontiguousarray(wg.transpose(1, 2, 0, 3))
    shared["wlru"] = _kp(w_lru_out, 8, 128)
    shared["wfo"] = _kp(w_f_out, 8, 128)
    shared["wout"] = np.ascontiguousarray(w_out.reshape(8, 128, D).transpose(1, 0, 2))
    shared["w1"] = _kp(w1, 32, 128)
    shared["w2"] = np.ascontiguousarray(w2.reshape(8, 4, 128, 2, 512).transpose(3, 0, 2, 1, 4))

    common_cols = [_fm(c_ctx), _fm(b_mod), _fm(g_mix), _fm(g_mlp),
                   np.concatenate([_fm(conv_w[k]) for k in range(4)], axis=1), _fm(conv_b),
                   np.concatenate([_fm(b_a[d]) for d in range(2)], axis=1),
                   np.concatenate([_fm(b_x[d]) for d in range(2)], axis=1),
                   np.concatenate([_fm(lam[d]) for d in range(2)], axis=1)]
    in_maps = []
    for b in range(B):
        m = dict(shared)
        m["x"] = np.ascontiguousarray(x[b])
        m["ctx"] = np.ascontiguousarray(ctx[b])
        m["vecs"] = np.ascontiguousarray(np.concatenate([_fm(c[b])] + common_cols, axis=1))
        assert m["vecs"].shape == (128, NV)
        in_maps.append(m)

    if "nc" not in _CACHE:
        _CACHE["nc"] = build_program(DEBUG)
    nc = _CACHE["nc"]
    res = run_bass_kernel_spmd(nc, in_maps, core_ids=list(range(B)))
    _CACHE["last"] = res
    out = np.stack([np.asarray(r["out"], np.float32) for r in res.results], axis=0)
    return out
```
